# Optimizing a Trainium2 kernel written in Bass

```python
import jax, jax.numpy as jnp
from jax import lax
import numpy as np

D_MODEL = 1024
BATCH = 2
SEQ = 16384
DEPTH = 2
DEC_BATCH = 32
DEC_SEQ = 64
PAST_LEN = 1024

CHUNK = 64
HEAD_DIM = 64
GROUP_WIDTH = D_MODEL // 4
MIX_WIDTH = 4 * GROUP_WIDTH
N_OUT_HEADS = MIX_WIDTH // HEAD_DIM
Q_BLOCK = 128
EPS = 1e-6
NEG = -1e30
A_HEADS = GROUP_WIDTH // HEAD_DIM
A_PREV_CHUNKS = 8
A_WIN = A_PREV_CHUNKS * CHUNK
A_BAND = (A_PREV_CHUNKS + 1) * CHUNK
REL_CLIP = 256
B_HEADS = 4
B_NOPE = 64
B_ROPE = 32
B_V = GROUP_WIDTH // B_HEADS
Q_LORA = D_MODEL // 4
KV_LORA = D_MODEL // 8
ROPE_THETA = 10000.0
MLA_SCALE = (B_NOPE + B_ROPE) ** -0.5
C_HEADS = GROUP_WIDTH // HEAD_DIM
M_HEADS = 4
N_MEM = 256

IN_SPLITS = ((GROUP_WIDTH,) * 4 + (Q_LORA, KV_LORA, B_ROPE, GROUP_WIDTH)
             + (GROUP_WIDTH,) * 4 + (GROUP_WIDTH,) * 2)
IN_WIDTH = sum(IN_SPLITS)

kernel_name = 'hybrid_streaming_encoder_step'


def rms(x, g):
    xf = x.astype(jnp.float32)
    y = xf * lax.rsqrt(jnp.mean(xf * xf, -1, keepdims=True) + EPS)
    return (y * g.astype(jnp.float32)).astype(x.dtype)


def heads(x, h):
    return x.reshape(x.shape[:-1] + (h, x.shape[-1] // h))


def split_in(h):
    return jnp.split(h, [int(i) for i in np.cumsum(IN_SPLITS)[:-1]], axis=-1)


def rope(x, pos):
    half = x.shape[-1] // 2
    freqs = ROPE_THETA ** (-jnp.arange(half, dtype=jnp.float32) / half)
    ang = pos.astype(jnp.float32)[:, None] * freqs[None, :]
    cos, sin = jnp.cos(ang)[:, None, :], jnp.sin(ang)[:, None, :]
    x1, x2 = x[..., :half].astype(jnp.float32), x[..., half:].astype(jnp.float32)
    return jnp.concatenate([x1 * cos - x2 * sin, x2 * cos + x1 * sin], -1).astype(x.dtype)


def rel_bias(tab, rel):
    return tab[:, jnp.clip(rel, -REL_CLIP, REL_CLIP) + REL_CLIP].astype(jnp.float32)


def dense_attn(q, k, v, scale):
    s = jnp.einsum('bqhd,bkhd->bhqk', q, k).astype(jnp.float32) * scale
    p = jax.nn.softmax(s, axis=-1)
    return jnp.einsum('bhqk,bkhd->bqhd', p.astype(v.dtype), v)


def band_attn_prompt(q, k, v, tab):
    B, S, H, Dh = q.shape
    nc = S // CHUNK
    qc = q.reshape(B, nc, CHUNK, H, Dh)

    def band(t):
        tp = jnp.concatenate([jnp.zeros((B, A_WIN, H, Dh), t.dtype), t], 1)
        tp = tp.reshape(B, nc + A_PREV_CHUNKS, CHUNK, H, Dh)
        return jnp.concatenate([tp[:, i:i + nc] for i in range(A_PREV_CHUNKS + 1)], 2)

    kb, vb = band(k), band(v)
    rel = (A_WIN + jnp.arange(CHUNK))[:, None] - jnp.arange(A_BAND)[None, :]
    kpos = jnp.arange(nc)[:, None] * CHUNK - A_WIN + jnp.arange(A_BAND)[None, :]
    s = jnp.einsum('bnqhd,bnkhd->bnhqk', qc, kb).astype(jnp.float32) * Dh ** -0.5 + rel_bias(tab, rel)
    s = jnp.where((kpos >= 0)[None, :, None, None, :], s, NEG)
    p = jax.nn.softmax(s, axis=-1)
    o = jnp.einsum('bnhqk,bnkhd->bnqhd', p.astype(v.dtype), vb)
    return o.reshape(B, S, H, Dh)


def band_attn_step(q, k, v, tab):
    T, Dh = q.shape[1], q.shape[-1]
    n_past = k.shape[1] - T
    rel = (n_past + jnp.arange(T))[:, None] - jnp.arange(n_past + T)[None, :]
    s = jnp.einsum('bqhd,bkhd->bhqk', q, k).astype(jnp.float32) * Dh ** -0.5 + rel_bias(tab, rel)
    p = jax.nn.softmax(s, axis=-1)
    return jnp.einsum('bhqk,bkhd->bqhd', p.astype(v.dtype), v)


def chunk_causal_attn_blocks(q, k, v, scale):
    B, S, H, Dq = q.shape
    nb = S // Q_BLOCK
    qb = q.reshape(B, nb, Q_BLOCK, H, Dq).transpose(1, 0, 2, 3, 4)
    kchunk = jnp.arange(S) // CHUNK

    def one(args):
        qi, b = args
        qchunk = (b * Q_BLOCK + jnp.arange(Q_BLOCK)) // CHUNK
        s = jnp.einsum('bqhd,bkhd->bhqk', qi, k).astype(jnp.float32) * scale
        s = jnp.where(kchunk[None, :] <= qchunk[:, None], s, NEG)
        p = jax.nn.softmax(s, axis=-1)
        return jnp.einsum('bhqk,bkhd->bqhd', p.astype(v.dtype), v)

    o = lax.map(one, (qb, jnp.arange(nb)))
    return o.transpose(1, 0, 2, 3, 4).reshape(B, S, H, v.shape[-1])


def stick_break(q, k, v, qpos, kpos):
    Dh = q.shape[-1]
    z = jnp.einsum('bqhd,bkhd->bhqk', q, k).astype(jnp.float32) * Dh ** -0.5
    mask = kpos[None, :] < qpos[:, None]
    log_1m = jnp.where(mask, jax.nn.log_sigmoid(-z), 0.0)
    tail = lax.cumsum(log_1m, axis=3, reverse=True) - log_1m
    w = jnp.where(mask, jnp.exp(jax.nn.log_sigmoid(z) + tail), 0.0)
    return jnp.einsum('bhqk,bkhd->bqhd', w.astype(v.dtype), v)


def stick_break_blocks(q, k, v):
    B, S, H, Dh = q.shape
    nb = S // Q_BLOCK
    qb = q.reshape(B, nb, Q_BLOCK, H, Dh).transpose(1, 0, 2, 3, 4)
    kpos = jnp.arange(S)

    def one(args):
        qi, b = args
        return stick_break(qi, k, v, b * Q_BLOCK + jnp.arange(Q_BLOCK), kpos)

    o = lax.map(one, (qb, jnp.arange(nb)))
    return o.transpose(1, 0, 2, 3, 4).reshape(B, S, H, Dh)


def branch_inputs(x, lp, pos):
    xn = rms(x, lp['norm_g'])
    aq, ak, av, ag, bcq, bckv, bkr, bg, cq, ck, cv, cg, mq, mg = split_in(xn @ lp['w_in'])
    qb = heads(rms(bcq, lp['b_cq_g']) @ lp['b_wq_b'], B_HEADS)
    q_nope = rms(qb[..., :B_NOPE], lp['b_qn_g'])
    q_rope = rope(rms(qb[..., B_NOPE:], lp['b_qr_g']), pos)
    return dict(
        aq=rms(heads(aq, A_HEADS), lp['a_qn_g']),
        ak=rms(heads(ak, A_HEADS), lp['a_kn_g']),
        av=heads(av, A_HEADS),
        bq=jnp.concatenate([q_nope, q_rope], -1),
        ckv=rms(bckv, lp['b_ckv_g']),
        krope=rope(rms(bkr, lp['b_kr_g'])[:, :, None, :], pos)[:, :, 0, :],
        cq=heads(cq, C_HEADS), ck=heads(ck, C_HEADS), cv=heads(cv, C_HEADS),
        mq=rms(heads(mq, M_HEADS), lp['m_qn_g']),
        gates=jnp.concatenate([ag, bg, cg, mg], -1),
    )


def mla_keys(ckv, krope, lp):
    kv = heads(ckv @ lp['b_wkv_b'], B_HEADS)
    k_nope = rms(kv[..., :B_NOPE], lp['b_kn_g'])
    kr = jnp.broadcast_to(krope[:, :, None, :], krope.shape[:2] + (B_HEADS, B_ROPE))
    return jnp.concatenate([k_nope, kr], -1), kv[..., B_NOPE:]


def mem_kv(mem, lp):
    k, v = jnp.split(rms(mem, lp['m_norm_g']) @ lp['w_mem_kv'], 2, axis=-1)
    return rms(heads(k, M_HEADS), lp['m_kn_g']), heads(v, M_HEADS)


def merge(x, outs, gates, lp):
    y = jnp.concatenate([o.reshape(o.shape[:2] + (-1,)) for o in outs], -1)
    y = rms(heads(y, N_OUT_HEADS), lp['out_g'].reshape(N_OUT_HEADS, HEAD_DIM))
    y = y.reshape(x.shape[:2] + (MIX_WIDTH,)) * jax.nn.silu(gates)
    return x + y @ lp['w_out']


def layer_prompt(x, mem, lp):
    S = x.shape[1]
    bi = branch_inputs(x, lp, jnp.arange(S))
    o_a = band_attn_prompt(bi['aq'], bi['ak'], bi['av'], lp['a_rel_bias'])
    kb, vb = mla_keys(bi['ckv'], bi['krope'], lp)
    o_b = chunk_causal_attn_blocks(bi['bq'], kb, vb, MLA_SCALE)
    o_c = stick_break_blocks(bi['cq'], bi['ck'], bi['cv'])
    mk, mv = mem_kv(mem, lp)
    o_m = dense_attn(bi['mq'], mk, mv, HEAD_DIM ** -0.5)
    y = merge(x, (o_a, o_b, o_c, o_m), bi['gates'], lp)
    keep = min(A_WIN, S)
    state = (bi['ak'][:, S - keep:], bi['av'][:, S - keep:], bi['ckv'], bi['krope'],
             bi['ck'], bi['cv'], mk, mv)
    return y, state


def layer_step(x, lp, ca_k, ca_v, cb_ckv, cb_kr, cc_k, cc_v, cm_k, cm_v):
    T = x.shape[1]
    n_past = cb_ckv.shape[1]
    pos = n_past + jnp.arange(T)
    bi = branch_inputs(x, lp, pos)
    ak_all = jnp.concatenate([ca_k, bi['ak']], 1)
    av_all = jnp.concatenate([ca_v, bi['av']], 1)
    o_a = band_attn_step(bi['aq'], ak_all, av_all, lp['a_rel_bias'])
    kb, vb = mla_keys(jnp.concatenate([cb_ckv, bi['ckv']], 1), jnp.concatenate([cb_kr, bi['krope']], 1), lp)
    o_b = dense_attn(bi['bq'], kb, vb, MLA_SCALE)
    o_c = stick_break(bi['cq'], jnp.concatenate([cc_k, bi['ck']], 1), jnp.concatenate([cc_v, bi['cv']], 1),
                      pos, jnp.arange(n_past + T))
    o_m = dense_attn(bi['mq'], cm_k, cm_v, HEAD_DIM ** -0.5)
    y = merge(x, (o_a, o_b, o_c, o_m), bi['gates'], lp)
    state = (ak_all[:, T:], av_all[:, T:], bi['ckv'], bi['krope'], bi['ck'], bi['cv'])
    return y, state


def setup_inputs(seed: int = 0) -> dict:
    key = jax.random.key(seed)
    ks = iter(jax.random.split(key, 40))

    def nrm(shape, scale):
        return jax.random.normal(next(ks), shape, jnp.float32) * scale

    def gain(shape):
        return 1.0 + nrm(shape, 0.02)

    n_a = min(A_WIN, PAST_LEN)
    return {
        'x_prompt': nrm((BATCH, SEQ, D_MODEL), 1.0),
        'x_sample': nrm((DEC_BATCH, DEC_SEQ, D_MODEL), 1.0),
        'mem_prompt': nrm((BATCH, N_MEM, D_MODEL), 1.0),
        'cache_a_k': nrm((DEPTH, DEC_BATCH, n_a, A_HEADS, HEAD_DIM), 1.0),
        'cache_a_v': nrm((DEPTH, DEC_BATCH, n_a, A_HEADS, HEAD_DIM), 1.0),
        'cache_b_ckv': nrm((DEPTH, DEC_BATCH, PAST_LEN, KV_LORA), 1.0),
        'cache_b_krope': nrm((DEPTH, DEC_BATCH, PAST_LEN, B_ROPE), 1.0),
        'cache_c_k': nrm((DEPTH, DEC_BATCH, PAST_LEN, C_HEADS, HEAD_DIM), 1.0),
        'cache_c_v': nrm((DEPTH, DEC_BATCH, PAST_LEN, C_HEADS, HEAD_DIM), 1.0),
        'cache_mem_k': nrm((DEPTH, DEC_BATCH, N_MEM, M_HEADS, HEAD_DIM), 1.0),
        'cache_mem_v': nrm((DEPTH, DEC_BATCH, N_MEM, M_HEADS, HEAD_DIM), 1.0),
        'norm_g': gain((DEPTH, D_MODEL)),
        'w_in': nrm((DEPTH, D_MODEL, IN_WIDTH), D_MODEL ** -0.5),
        'a_qn_g': gain((DEPTH, HEAD_DIM)),
        'a_kn_g': gain((DEPTH, HEAD_DIM)),
        'a_rel_bias': nrm((DEPTH, A_HEADS, 2 * REL_CLIP + 1), 0.1),
        'b_cq_g': gain((DEPTH, Q_LORA)),
        'b_wq_b': nrm((DEPTH, Q_LORA, B_HEADS * (B_NOPE + B_ROPE)), Q_LORA ** -0.5),
        'b_ckv_g': gain((DEPTH, KV_LORA)),
        'b_wkv_b': nrm((DEPTH, KV_LORA, B_HEADS * (B_NOPE + B_V)), KV_LORA ** -0.5),
        'b_qn_g': gain((DEPTH, B_NOPE)),
        'b_qr_g': gain((DEPTH, B_ROPE)),
        'b_kn_g': gain((DEPTH, B_NOPE)),
        'b_kr_g': gain((DEPTH, B_ROPE)),
        'm_norm_g': gain((DEPTH, D_MODEL)),
        'w_mem_kv': nrm((DEPTH, D_MODEL, 2 * M_HEADS * HEAD_DIM), D_MODEL ** -0.5),
        'm_qn_g': gain((DEPTH, HEAD_DIM)),
        'm_kn_g': gain((DEPTH, HEAD_DIM)),
        'out_g': gain((DEPTH, MIX_WIDTH)),
        'w_out': nrm((DEPTH, MIX_WIDTH, D_MODEL), 0.5 * MIX_WIDTH ** -0.5),
    }


def reference(x_prompt, x_sample, mem_prompt, cache_a_k, cache_a_v, cache_b_ckv, cache_b_krope,
              cache_c_k, cache_c_v, cache_mem_k, cache_mem_v, norm_g, w_in, a_qn_g, a_kn_g, a_rel_bias,
              b_cq_g, b_wq_b, b_ckv_g, b_wkv_b, b_qn_g, b_qr_g, b_kn_g, b_kr_g, m_norm_g, w_mem_kv,
              m_qn_g, m_kn_g, out_g, w_out):
    hp, hs = x_prompt, x_sample
    p_states, s_states = [], []
    for l in range(DEPTH):
        lp = dict(norm_g=norm_g[l], w_in=w_in[l], a_qn_g=a_qn_g[l], a_kn_g=a_kn_g[l],
                  a_rel_bias=a_rel_bias[l], b_cq_g=b_cq_g[l], b_wq_b=b_wq_b[l], b_ckv_g=b_ckv_g[l],
                  b_wkv_b=b_wkv_b[l], b_qn_g=b_qn_g[l], b_qr_g=b_qr_g[l], b_kn_g=b_kn_g[l],
                  b_kr_g=b_kr_g[l], m_norm_g=m_norm_g[l], w_mem_kv=w_mem_kv[l], m_qn_g=m_qn_g[l],
                  m_kn_g=m_kn_g[l], out_g=out_g[l], w_out=w_out[l])
        hp, sp = layer_prompt(hp, mem_prompt, lp)
        hs, ss = layer_step(hs, lp, cache_a_k[l], cache_a_v[l], cache_b_ckv[l], cache_b_krope[l],
                            cache_c_k[l], cache_c_v[l], cache_mem_k[l], cache_mem_v[l])
        p_states.append(sp)
        s_states.append(ss)
    a_k_p, a_v_p, ckv_p, kr_p, c_k_p, c_v_p, mem_k_p, mem_v_p = [
        jnp.stack([st[i] for st in p_states]) for i in range(8)]
    a_k_s, a_v_s, ckv_s, kr_s, c_k_s, c_v_s = [
        jnp.stack([st[i] for st in s_states]) for i in range(6)]
    return (hp, hs, a_k_p, a_v_p, ckv_p, kr_p, c_k_p, c_v_p, mem_k_p, mem_v_p,
            a_k_s, a_v_s, ckv_s, kr_s, c_k_s, c_v_s)
```

```python
import os
from contextlib import ExitStack
import numpy as np
import concourse.bass as bass
import concourse.mybir as mybir
from concourse.bass_utils import run_bass_kernel_spmd

F32 = mybir.dt.float32
F32R = mybir.dt.float32r
BF16 = mybir.dt.bfloat16
AF = mybir.ActivationFunctionType
ALU = mybir.AluOpType
AX = mybir.AxisListType

D = 1024
NSEQ = 16
TS = 64
EPS = 1e-6
MLA_SCALE = 96.0 ** -0.5
NCOL = 1120
NST = 416
GV = 576
G_AQ, G_MQ, G_AK, G_BQN, G_BQR, G_BKN, G_BKR, G_CKV, G_MK = 0, 64, 128, 192, 256, 288, 352, 384, 512
SAME_ENGINE_SYNC = os.environ.get("K_SES", "1") == "1"
NDS = 24


class Op:
    __slots__ = ("eng", "fn", "deps", "sig", "sem", "val", "dma", "needs", "inc")

    def __init__(self, eng, fn, dma):
        self.eng, self.fn, self.dma = eng, fn, dma
        self.deps = set()
        self.sig = None
        self.sem = None
        self.val = None
        self.needs = False
        self.inc = 16 if dma else 1


class Prog:
    ENGS = ("sp", "act", "dve", "pool", "pe")

    def __init__(self):
        self.ops = {e: [] for e in self.ENGS}
        self.order = []
        self.lw = {}
        self.rd = {}

    def add(self, eng, fn, r=(), w=(), dma=False, extra=()):
        op = Op(eng, fn, dma)
        deps = set(extra)
        for b in r:
            o = self.lw.get(b)
            if o is not None:
                deps.add(o)
        for b in w:
            o = self.lw.get(b)
            if o is not None:
                deps.add(o)
            rr = self.rd.get(b)
            if rr:
                deps.update(rr.values())
        for b in w:
            self.lw[b] = op
            self.rd[b] = {}
        for b in r:
            d = self.rd.setdefault(b, {})
            d[id(op) if dma else eng] = op
        deps.discard(op)
        op.deps = deps
        self.ops[eng].append(op)
        self.order.append(op)
        return op

    def finalize(self, esem, dsems):
        last_on = {}
        cnt = 0
        tot = {}
        for op in self.order:
            if op.dma and op.sem is None:
                s = dsems[cnt % len(dsems)]
                cnt += 1
                if s in last_on:
                    op.deps.add(last_on[s])
                last_on[s] = op
                tot[s] = tot.get(s, 0) + op.inc
                op.sem, op.val = s, tot[s]
            elif op.dma:
                tot[op.sem] = tot.get(op.sem, 0) + op.inc
                op.val = tot[op.sem]
        for op in self.order:
            for d in op.deps:
                if not d.dma:
                    if d.eng == op.eng and (d.eng == "pe" or not SAME_ENGINE_SYNC):
                        continue
                    d.needs = True
        for e in self.ENGS:
            n = 0
            for op in self.ops[e]:
                if not op.dma and op.needs:
                    n += 1
                    op.sig = n
                    op.sem = esem[e]
        self.final = dict(tot)

    def emit(self, ename, eng):
        waited = {}
        for op in self.ops[ename]:
            need = {}
            for d in op.deps:
                if d.dma:
                    key, v = d.sem, d.val
                else:
                    if d.sig is None:
                        continue
                    key, v = d.sem, d.sig
                if waited.get(key, 0) >= v:
                    continue
                if need.get(key, 0) < v:
                    need[key] = v
            for key, v in need.items():
                eng.wait_ge(key, v)
                waited[key] = v
            inst = op.fn(eng)
            if op.dma and op.inc == 1:
                inst.then_inc(op.sem)
            elif op.dma:
                inst.then_inc(op.sem, op.inc)
            elif op.needs:
                inst.then_inc(op.sem, 1)
        if ename == "sp":
            for s, v in self.final.items():
                if waited.get(s, 0) < v:
                    eng.wait_ge(s, v)


def build(S=16384, L=2, debug=False):
    T = S + NSEQ * TS
    NT = T // 128
    NSUP = T // 512
    NPT = S // 128
    NPS = S // 512
    CB0 = 0 if NPT >= 128 else NT
    NB = max(NT, CB0 + 128)
    MB0 = 2

    nc = bass.Bass("TRN2", target_bir_lowering=False, dynamic_dma_scratch_size=1024)
    es = ExitStack()
    with es:
        try:
            es.enter_context(nc.allow_low_precision("bf16 matmul operands, fp32 accumulation"))
        except Exception:
            pass
        try:
            es.enter_context(nc.allow_non_contiguous_dma("toeplitz / strided layouts"))
        except Exception:
            pass

        def din(name, shape, dt=F32):
            return nc.dram_tensor(name, list(shape), dt, kind="ExternalInput").ap()

        def dout(name, shape, dt=F32):
            return nc.dram_tensor(name, list(shape), dt, kind="ExternalOutput").ap()

        x_in = din("x", [T, D])
        mem_in = din("mem", [256, D])
        cs_in = din("cs", [T, 64])
        w_in_d = din("w_in", [L, D, NCOL])
        ng_d = din("ng", [L, 128, 8])
        wq_d = din("wq", [L, 256, 96])
        cqg_d = din("cqg", [L, 128, 2])
        wkv_d = din("wkv", [L, 128, 128])
        wmem_d = din("wmem", [L, D, 128])
        mng_d = din("mng", [L, 128, 8])
        wout_d = din("wout", [L, D, D])
        gv_d = din("gv", [L, GV])
        outg_d = din("outg", [L, 64, 4])
        tab_d = din("tab", [L, 513])
        c_ak = din("c_ak", [L, NSEQ, 512, 64])
        c_av = din("c_av", [L, NSEQ, 512, 64])
        c_ck = din("c_ck", [L, NSEQ, 1024, 64])
        c_cv = din("c_cv", [L, NSEQ, 1024, 64])
        c_ckv = din("c_ckv", [L, NSEQ, 1024, 128])
        c_kr = din("c_kr", [L, NSEQ, 1024, 32])
        c_mk = din("c_mk", [L, NSEQ, 256, 64])
        c_mv = din("c_mv", [L, NSEQ, 256, 64])

        y_out = dout("y", [T, D])
        st_out = dout("o_state", [L, T, NST])
        mem_out = dout("o_mem", [L, 256, 128])
        aks_out = dout("o_aks", [L, NSEQ, 448, 64])
        avs_out = dout("o_avs", [L, NSEQ, 448, 64])

        xmid = nc.dram_tensor("xmid", [T, D], F32).ap()
        yT_loc = nc.dram_tensor("yT_loc", [NSUP * 256, 512], BF16).ap()
        yT_all = nc.dram_tensor("yT_all", [NSUP * 1024, 512], BF16).ap()
        ext_d = nc.dram_tensor("ext_d", [L, 2048], F32).ap()
        zt_d = nc.dram_tensor("zt_d", [128, 1921], F32).ap()

        def sb(name, shape, dt):
            return es.enter_context(nc.sbuf_tensor(name, list(shape), dt))

        def ps(name):
            return es.enter_context(nc.psum_tensor(name, [128, 512], F32))

        KTAC = sb("KTAC", [128, NB * 128], BF16)
        KTB = sb("KTB", [128, NB * 128], BF16)
        VA = sb("VA", [128, NB + 1, 64], BF16)
        VB = sb("VB", [128, NB + 1, 64], BF16)
        VC = sb("VC", [128, NB + 1, 64], BF16)
        KTM = sb("KTM", [64, 18 * 128], BF16)
        VM = sb("VM", [128, 18, 64], BF16)
        w_in_sb = sb("w_in_sb", [128, 8, NCOL], BF16)
        wmem_sb = sb("wmem_sb", [128, 8, 128], BF16)
        wq_sb = sb("wq_sb", [128, 2, 96], BF16)
        wkv_sb = sb("wkv_sb", [128, 128], BF16)
        wout_sb = KTB[:, 0:8192].rearrange("p (k c) -> p k c", k=8)
        YTs = [KTB[:, 8192 + i * 4096: 8192 + (i + 1) * 4096].rearrange("p (k c) -> p k c", k=8) for i in range(2)]
        MA = [sb(f"MA{i}", [128, 512], BF16) for i in range(8)]
        MAs1 = sb("MAs1", [128, 64], BF16)
        MCs1 = sb("MCs1", [128, 64], BF16)
        HS = [sb(f"HS{i}", [128, 64], BF16) for i in range(2)]
        TRI = sb("TRI", [128, 128], BF16)
        BLK = sb("BLK", [128, 128], BF16)
        negtri = sb("negtri", [128, 128], F32R)
        negones = sb("negones", [128, 128], F32R)
        ones_bf = sb("ones_bf", [128, 128], BF16)
        ones_f = sb("ones_f", [128, 128], F32)
        ident_bf = sb("ident_bf", [128, 128], BF16)
        ident_f = sb("ident_f", [128, 128], F32)
        invn = sb("invn", [128, 8], F32)
        invn2 = sb("invn2", [128, 4], F32)
        shift = sb("shift", [128, 4], F32)
        gvb = sb("gvb", [128, GV], F32)
        outg_sb = sb("outg_sb", [64, 4], F32)
        ngs = sb("ngs", [128, 8], F32)
        mngs = sb("mngs", [128, 8], F32)
        cqgs = sb("cqgs", [128, 2], F32)
        exrow = KTAC[0:1, 0:4096].bitcast(F32)
        zrow = KTAC[0:1, 4096:5632].bitcast(F32)
        exkeys = [("KTAC", j) for j in range(44)]

        XT = [sb(f"xt{i}", [128, D], F32) for i in range(2)]
        CS = [sb(f"cs{i}", [128, 64], F32) for i in range(2)]
        xn = sb("xn", [128, D], BF16)
        junk = xn
        xnT = sb("xnT", [128, D], BF16)
        ST = [sb(f"st{i}", [128, NST], F32) for i in range(1)]
        stat = sb("stat", [128, 8], F32)
        statm = sb("statm", [128, 8], F32)
        lnv = sb("lnv", [128, 8], F32)
        rstd = sb("rstd", [128, 8], F32)
        stat2 = sb("stat2", [128, 4], F32)
        statm2 = sb("statm2", [128, 4], F32)
        lnv2 = sb("lnv2", [128, 4], F32)
        rstd2 = sb("rstd2", [128, 4], F32)
        sqs = sb("sqs", [128, 416], F32)
        wsm = [sqs[:, 96 * i:96 * (i + 1)] for i in range(2)]
        tmpf = sqs[:, 192:320]
        sq0 = sqs[:, 0:256]
        sq1 = sqs
        sq2 = sqs[:, 0:96]
        qtok = sb("qtok", [128, 192], BF16)
        kactok = sb("kactok", [128, 128], BF16)
        cqn = sb("cqn", [128, 256], BF16)
        cqnT = sb("cqnT", [128, 256], BF16)
        ckvb = sb("ckvb", [128, 128], BF16)
        ckvT = sb("ckvT", [128, 128], BF16)
        krn = sb("krn", [128, 32], F32)
        qrn = sb("qrn", [128, 32], F32)
        rpa = sb("rpa", [128, 32], F32)
        rpb = sb("rpb", [128, 32], F32)
        kbtok = sb("kbtok", [128, 96], BF16)
        qbtok = sb("qbtok", [128, 96], BF16)
        ge = sb("ge", [128, 256], F32)
        gtok = sb("gtok", [128, 256], BF16)
        kmtok = sb("kmtok", [128, 64], BF16)
        stm = sb("stm", [128, 128], F32)

        NBUF = 2 if NPT >= 128 else int(os.environ.get("K_NBUF", "1"))
        NB1 = 2 if NPT >= 128 else 1
        QTAC = [sb(f"QTAC{i}", [128, 512], BF16) for i in range(NBUF)]
        QTB = [sb(f"QTB{i}", [128, 512], BF16) for i in range(NBUF)]
        QTM = [sb(f"QTM{i}", [64, 512], BF16) for i in range(NBUF)]
        GT = [sb(f"GT{i}", [64, 4, 512], BF16) for i in range(NBUF)]
        Eb = [sb(f"E{i}", [128, 512], F32) for i in range(NB1)]
        SPb = [sb(f"SP{i}", [128, 512], F32R) for i in range(NB1)]
        Rbb = [sb(f"R{i}", [128, 512], F32R) for i in range(2)]
        Wb = [sb(f"W{i}", [128, 512], BF16) for i in range(NB1)]
        Pb = [sb(f"P{i}", [128, 512], BF16) for i in range(2)]
        psq = sb("psq", [64, 512], F32)
        pd = sb("pd", [64, 512], F32)
        yTb = [sb(f"yT{i}", [64, 512], BF16) for i in range(1)]
        Pacc = sb("Pacc", [128, 512], F32)
        XO = XT
        kcst = sb("kcst", [128, 2, 128], F32)
        vast = sb("vast", [128, 2, 64], F32)
        vcst = sb("vcst", [128, 2, 64], F32)
        ckvst = sb("ckvst", [128, 2, 128], F32)
        krst = sb("krst", [128, 2, 32], F32)
        mkst = sb("mkst", [128, 2, 64], F32)
        mvst = sb("mvst", [128, 2, 64], F32)
        tz = Eb[0]
        vis = Wb[0]

        psP = [ps(f"psP{i}") for i in range(3)]
        psS = [ps(f"psS{i}") for i in range(2)]
        psAcc = ps("psAcc")
        psO = ps("psO")
        psD = ps("psD")

        esem = {e: es.enter_context(nc.semaphore(f"sem_{e}")) for e in Prog.ENGS}
        dsems = [es.enter_context(nc.semaphore(f"dsem{i}")) for i in range(NDS)]
        ccsem = es.enter_context(nc.semaphore("ccsem"))

        P = Prog()
        try:
            print("sbuf bytes remaining", nc.sbuf_bytes_remaining, flush=True)
        except Exception as ex:
            print("sbuf remaining n/a", ex)

        def ACT(out, in_, func, r, w, **kw):
            return P.add("act", lambda e: e.activation(out=out, in_=in_, func=func, **kw), r, w)

        def TSC(eng, out, in0, s1, s2, op0, op1, r, w):
            if op1 is None:
                return P.add(eng, lambda e: e.tensor_scalar(out=out, in0=in0, scalar1=s1, scalar2=None, op0=op0), r, w)
            return P.add(eng, lambda e: e.tensor_scalar(out=out, in0=in0, scalar1=s1, scalar2=s2, op0=op0, op1=op1), r, w)

        def STT(out, in0, sc, in1, op0, op1, r, w):
            return P.add("dve", lambda e: e.scalar_tensor_tensor(out=out, in0=in0, scalar=sc, in1=in1, op0=op0, op1=op1), r, w)

        def TT(eng, out, in0, in1, op, r, w):
            return P.add(eng, lambda e: e.tensor_tensor(out=out, in0=in0, in1=in1, op=op), r, w)

        def CP(eng, out, in_, r, w):
            if eng == "act":
                return ACT(out, in_, AF.Copy, r, w)
            return P.add(eng, lambda e: e.tensor_copy(out=out, in_=in_), r, w)

        def MS(eng, out, val, w):
            return P.add(eng, lambda e: e.memset(out, val), (), w)

        def RED(out, in_, r, w):
            return P.add("dve", lambda e: e.tensor_reduce(out=out, in_=in_, axis=AX.X, op=ALU.add), r, w)

        def MMG(mms, r, w):
            def fn(e):
                inst = None
                for (o, l, rh, st_, sp_) in mms:
                    inst = e.matmul(o, lhsT=l, rhs=rh, start=st_, stop=sp_, skip_group_check=True)
                return inst
            return P.add("pe", fn, r, w)

        def TRG(trs, r, w):
            def fn(e):
                inst = None
                for (o, i_, idn) in trs:
                    inst = e.transpose(out=o, in_=i_, identity=idn)
                return inst
            return P.add("pe", fn, r, w)

        def DMA(out, in_, r, w, extra=()):
            return P.add("sp", lambda e: e.dma_start(out=out, in_=in_), r, w, dma=True, extra=extra)

        _pp = [0]

        def nextP():
            _pp[0] = (_pp[0] + 1) % 3
            return psP[_pp[0]], ("psP", _pp[0])

        def bfv(pt):
            return pt[:].bitcast(BF16)

        MS("pool", ones_f[:], 1.0, ["ones_f"])
        MS("pool", ones_bf[:], 1.0, ["ones_bf"])
        MS("pool", tmpf, -1.0, ["sqs"])
        P.add("pool", lambda e: e.affine_select(out=ident_f[:], in_=ones_f[:], pattern=[[1, 128]], compare_op=ALU.is_equal,
                                                fill=0.0, base=0, channel_multiplier=-1), ["ones_f"], ["ident_f"])
        CP("dve", ident_bf[:], ident_f[:], ["ident_f"], ["ident_bf"])
        P.add("pool", lambda e: e.affine_select(out=TRI[:], in_=ones_bf[:], pattern=[[1, 128]], compare_op=ALU.is_gt,
                                                fill=0.0, base=0, channel_multiplier=-1), ["ones_bf"], ["TRI"])
        CP("dve", BLK[:], ones_bf[:], ["ones_bf"], ["BLK"])
        MS("dve", BLK[64:128, 0:64], 0.0, ["BLK"])
        CP("dve", negones[:], tmpf, ["sqs"], ["negones"])
        P.add("pool", lambda e: e.affine_select(out=ones_f[:], in_=tmpf, pattern=[[-1, 128]], compare_op=ALU.is_ge,
                                                fill=0.0, base=0, channel_multiplier=1), ["sqs", "ones_f"], ["ones_f"])
        CP("dve", negtri[:], ones_f[:], ["ones_f"], ["negtri"])
        MS("pool", ones_f[:], 1.0, ["ones_f"])
        for i, v in enumerate([1.0 / 1024, 1.0 / 64, 1.0 / 64, 1.0 / 64, 1.0 / 64, 1.0 / 256, 1.0 / 128, 1.0 / 32]):
            MS("dve", invn[:, i:i + 1], v, ["invn"])
        for i, v in enumerate([1.0 / 64, 1.0 / 32, 1.0 / 64, 1.0]):
            MS("dve", invn2[:, i:i + 1], v, ["invn2"])
        for i, v in enumerate([-4.0, -5.0, 0.0, -4.0]):
            MS("dve", shift[:, i:i + 1], v, ["shift"])
        MS("dve", HS[0][:], 0.0, ["HS0"])
        MS("dve", HS[0][0:64, :], 1.0, ["HS0"])
        MS("dve", HS[1][:], 0.0, ["HS1"])
        MS("dve", HS[1][64:128, :], 1.0, ["HS1"])
        CP("dve", MCs1[:], TRI[:, 64:128], ["TRI"], ["MCs1"])
        MS("dve", MCs1[0:64, :], 0.0, ["MCs1"])
        MS("dve", stat2[:], 1.0, ["stat2"])
        for vt_, vn_ in ((VA, "VA"), (VB, "VB"), (VC, "VC")):
            MS("pool", vt_[:, NB, :], 0.0, [(vn_, NB)])
        MS("pool", KTB[96:128, :], 0.0, [("KTB", j) for j in range(NB)])
        for i in range(NBUF):
            MS("pool", QTB[i][96:128, :], 0.0, [("QTB", i, u) for u in range(4)])
        for i in range(2):
            MS("dve", Rbb[i][:].bitcast(F32), 0.0, [("R", i)])

        def rstd_round(st_t, stm_t, ln_t, rs_t, inv_t, n, keys):
            ks, km, kl, kr_ = keys
            TT("dve", stm_t[:, 0:n], st_t[:, 0:n], inv_t[:, 0:n], ALU.mult, [ks], [km])
            ACT(ln_t[:, 0:n], stm_t[:, 0:n], AF.Ln, [km], [kl], bias=EPSB[:, 0:1])
            ACT(rs_t[:, 0:n], ln_t[:, 0:n], AF.Exp, [kl], [kr_], scale=-0.5)

        EPSB = sb("epsb", [128, 1], F32)
        MS("dve", EPSB[:], EPS, ["epsb"])

        def rope(src, cst, dst, rkeys, wkeys):
            TT("pool", rpa[:], src[:, 0:32], cst[:, 0:32], ALU.mult, rkeys, ["rpa"])
            TT("pool", rpb[:, 0:16], src[:, 16:32], cst[:, 32:48], ALU.mult, rkeys, ["rpb0"])
            TT("pool", rpb[:, 16:32], src[:, 0:16], cst[:, 48:64], ALU.mult, rkeys, ["rpb1"])
            TT("pool", dst, rpa[:], rpb[:], ALU.add, ["rpa", "rpb0", "rpb1"], wkeys)

        def mla_kv(ckv_f32, kr_f32, rk, blk, kcol0):
            CP("pool", ckvb[:], ckv_f32, rk, ["ckvb"])
            pt, pk = nextP()
            TRG([(bfv(pt)[:, 0:128], ckvb[:], ident_bf[:])], ["ckvb", "ident_bf"], [pk])
            CP("dve", ckvT[:], bfv(pt)[:, 0:128], [pk], ["ckvT"])
            pt2, pk2 = nextP()
            MMG([(pt2[:, 0:128], ckvT[:], wkv_sb[:], True, True)], ["ckvT", "wkv_sb"], [pk2])
            ACT(sq2[:, 0:64], pt2[:, 0:64], AF.Square, [pk2], ["sqs", "stat2c"], accum_out=stat2[:, 2:3])
            TSC("dve", statm2[:, 2:3], stat2[:, 2:3], 1.0 / 64, None, ALU.mult, None, ["stat2c"], ["statm2c"])
            ACT(lnv2[:, 2:3], statm2[:, 2:3], AF.Ln, ["statm2c"], ["lnv2c"], bias=EPSB[:, 0:1])
            ACT(rstd2[:, 2:3], lnv2[:, 2:3], AF.Exp, ["lnv2c"], ["rstd2c"], scale=-0.5)
            STT(kbtok[:, 0:64], pt2[:, 0:64], rstd2[:, 2:3], gvb[:, G_BKN:G_BKN + 64], ALU.mult, ALU.mult,
                [pk2, "rstd2c", "gvb"], ["kbtok_n"])
            CP("dve", VB[:, blk, :], pt2[:, 64:128], [pk2], [("VB", blk)])
            CP("pool", kbtok[:, 64:96], kr_f32, rk, ["kbtok_r"])
            pt3, pk3 = nextP()
            TRG([(bfv(pt3)[0:96, 0:128], kbtok[:], ident_bf[:])], ["kbtok_n", "kbtok_r", "ident_bf"], [pk3])
            jb = kcol0 // 128
            ali = ["wout_sb"] if jb < 64 else ([("YT", 0)] if jb < 96 else ([("YT", 1)] if jb < 128 else []))
            CP("dve", KTB[0:96, kcol0:kcol0 + 128], bfv(pt3)[0:96, 0:128], [pk3], [("KTB", jb)] + ali)

        def token_norm_T(src_f32, skey):
            ACT(junk[:], src_f32, AF.Square, [skey], ["junk", "stat_x"], accum_out=stat[:, 0:1])
            ACT(lnv[:, 0:1], stat[:, 0:1], AF.Ln, ["stat_x"], ["lnv_x"], scale=1.0 / 1024, bias=EPSB[:, 0:1])
            ACT(rstd[:, 0:1], lnv[:, 0:1], AF.Exp, ["lnv_x"], ["rstd_x"], scale=-0.5)
            TSC("dve", xn[:], src_f32, rstd[:, 0:1], None, ALU.mult, None, [skey, "rstd_x"], ["xn"])
            pt, pk = nextP()
            TRG([(bfv(pt)[:, 128 * k:128 * (k + 1)], xn[:, 128 * k:128 * (k + 1)], ident_bf[:]) for k in range(8)],
                ["xn", "ident_bf"], [pk])
            CP("dve", xnT[:], bfv(pt)[:, 0:1024], [pk], ["xnT"])

        prev_ag = None
        for l in range(L):
            xsrc = x_in if l == 0 else xmid
            xdst = xmid if l < L - 1 else y_out

            DMA(ngs[:], ng_d[l], [], ["ngs"])
            DMA(mngs[:], mng_d[l], [], ["mngs"])
            DMA(cqgs[:], cqg_d[l], [], ["cqgs"])
            DMA(gvb[:], gv_d[l].partition_broadcast(128), [], ["gvb"])
            DMA(outg_sb[:], outg_d[l], [], ["outg"])
            TSC("dve", gvb[:, 0:128], gvb[:, 0:128], 0.125, None, ALU.mult, None, ["gvb"], ["gvb"])
            TSC("dve", gvb[:, G_BQN:G_BQN + 96], gvb[:, G_BQN:G_BQN + 96], MLA_SCALE, None, ALU.mult, None, ["gvb"], ["gvb"])
            for k in range(8):
                w_ = XT[k % 2]
                wk = ("xt", k % 2)
                DMA(w_[:, 0:1024], w_in_d[l, 128 * k:128 * (k + 1), 0:1024], [], [wk])
                DMA(wsm[k % 2], w_in_d[l, 128 * k:128 * (k + 1), 1024:NCOL], [], ["sqs"])
                TSC("dve", w_in_sb[:, k, 0:1024], w_[:, 0:1024], ngs[:, k:k + 1], None, ALU.mult, None, [wk, "ngs"], ["w_in_sb"])
                TSC("dve", w_in_sb[:, k, 1024:NCOL], wsm[k % 2], ngs[:, k:k + 1], None, ALU.mult, None, ["sqs", "ngs"], ["w_in_sb"])
            for k in range(8):
                w_ = XT[k % 2]
                wk = ("xt", k % 2)
                DMA(w_[:, 0:128], wmem_d[l, 128 * k:128 * (k + 1), :], [], [wk])
                TSC("dve", wmem_sb[:, k, :], w_[:, 0:128], mngs[:, k:k + 1], None, ALU.mult, None, [wk, "mngs"], ["wmem_sb"])
            for k in range(2):
                w_ = XT[k % 2]
                wk = ("xt", k % 2)
                DMA(w_[:, 0:96], wq_d[l, 128 * k:128 * (k + 1), :], [], [wk])
                TSC("dve", wq_sb[:, k, :], w_[:, 0:96], cqgs[:, k:k + 1], None, ALU.mult, None, [wk, "cqgs"], ["wq_sb"])
            DMA(XT[0][:, 0:128], wkv_d[l], [], [("xt", 0)])
            CP("dve", wkv_sb[:], XT[0][:, 0:128], [("xt", 0)], ["wkv_sb"])

            P.add("dve", lambda e: e.memset(zrow[0:1, 0:768], 0.0), (), exkeys, extra=([prev_ag] if prev_ag is not None else []))
            DMA(exrow[0:1, 768:1281], tab_d[l:l + 1, :], [], exkeys)
            TSC("dve", exrow[0:1, 0:768], zrow[0:1, 0:768], exrow[0:1, 768:769], None, ALU.add, None, exkeys, exkeys)
            TSC("dve", exrow[0:1, 1281:2048], zrow[0:1, 0:767], exrow[0:1, 1280:1281], None, ALU.add, None, exkeys, exkeys)
            ACT(exrow[0:1, :], exrow[0:1, :], AF.Exp, exkeys, exkeys)
            DMA(ext_d[l:l + 1, :], exrow[0:1, :], exkeys, ["ext_d"] + exkeys)
            Jm = sqs[:, 0:128]
            P.add("pool", lambda e: e.affine_select(out=Jm, in_=ones_f[:], pattern=[[1, 128]], compare_op=ALU.is_equal,
                                                    fill=0.0, base=-127, channel_multiplier=1), ["ones_f", "sqs"], ["sqs"])
            for r in range(-4, 4):
                src = bass.AP(ext_d.tensor, l * 2048 + 1024 - 128 * r - 127, [[1, 128], [1, 512]])
                DMA(tz[:], src, ["ext_d"], [("E", 0)])
                ptz, pkz = nextP()
                MMG([(ptz[:, :], Jm, tz[:], True, True)], ["sqs", ("E", 0)], [pkz])
                MS("pool", vis[:], 0.0, [("W", 0)])
                for hk in range(2):
                    ck = 2 * r + hk
                    lo, hi = max(ck, 0), min(ck + 8, 7)
                    if lo <= hi:
                        MS("pool", vis[64 * hk:64 * hk + 64, 64 * lo:64 * (hi + 1)], 1.0, [("W", 0)])
                TT("dve", MA[r + 4][:], ptz[:, :], vis[:], ALU.mult, [pkz, ("W", 0)], [("MA", r + 4)])
            CP("dve", MAs1[:], MA[4][:, 64:128], [("MA", 4)], ["MAs1"])
            MS("dve", MAs1[0:64, :], 0.0, ["MAs1"])

            for mt in range(2):
                xt = XT[mt % 2]
                xk = ("xt", mt % 2)
                DMA(xt[:], mem_in[128 * mt:128 * (mt + 1), :], [], [xk])
                token_norm_T(xt[:], xk)
                pt, pk = nextP()
                MMG([(pt[:, 0:128], xnT[:, 128 * k:128 * (k + 1)], wmem_sb[:, k, :], k == 0, k == 7) for k in range(8)],
                    ["xnT", "wmem_sb"], [pk])
                ACT(sq2[:, 0:64], pt[:, 0:64], AF.Square, [pk], ["sqs", "stat2c"], accum_out=stat2[:, 2:3])
                TSC("dve", statm2[:, 2:3], stat2[:, 2:3], 1.0 / 64, None, ALU.mult, None, ["stat2c"], ["statm2c"])
                ACT(lnv2[:, 2:3], statm2[:, 2:3], AF.Ln, ["statm2c"], ["lnv2c"], bias=EPSB[:, 0:1])
                ACT(rstd2[:, 2:3], lnv2[:, 2:3], AF.Exp, ["lnv2c"], ["rstd2c"], scale=-0.5)
                STT(stm[:, 0:64], pt[:, 0:64], rstd2[:, 2:3], gvb[:, G_MK:G_MK + 64], ALU.mult, ALU.mult,
                    [pk, "rstd2c", "gvb"], ["stm_k"])
                CP("dve", stm[:, 64:128], pt[:, 64:128], [pk], ["stm_v"])
                DMA(mem_out[l, 128 * mt:128 * (mt + 1), :], stm[:], ["stm_k", "stm_v"], [("mem_out", l, mt)])
                CP("pool", kmtok[:], stm[:, 0:64], ["stm_k"], ["kmtok"])
                CP("pool", VM[:, mt, :], stm[:, 64:128], ["stm_v"], [("VM", mt)])
                pt2, pk2 = nextP()
                TRG([(bfv(pt2)[0:64, 0:128], kmtok[:], ident_bf[:])], ["kmtok", "ident_bf"], [pk2])
                CP("dve", KTM[0:64, 128 * mt:128 * (mt + 1)], bfv(pt2)[0:64, 0:128], [pk2], [("KTM", mt)])

            DMA(aks_out[l], c_ak[l, :, 64:512, :], [], [("aks", l)])
            DMA(avs_out[l], c_av[l, :, 64:512, :], [], [("avs", l)])

            def load_tile(i):
                xt = XT[i % 2]
                DMA(xt[:], xsrc[128 * i:128 * (i + 1), :], [("xdram", i)], [("xt", i % 2)])
                DMA(CS[i % 2][:], cs_in[128 * i:128 * (i + 1), :], [], [("cs", i % 2)])

            def project_tile(i, sp_, u):
                xt = XT[i % 2]
                xk = ("xt", i % 2)
                cst = CS[i % 2]
                ck_ = ("cs", i % 2)
                st = ST[i % len(ST)]
                sk = ("st", i % len(ST))
                blk = i
                c0 = 128 * blk
                token_norm_T(xt[:], xk)
                if i + 1 < NT:
                    load_tile(i + 1)
                yield
                g0, k0 = nextP()
                MMG([(g0[:, 0:448], xnT[:, 128 * k:128 * (k + 1)], w_in_sb[:, k, 0:448], k == 0, k == 7) for k in range(8)],
                    ["xnT", "w_in_sb"], [k0])
                g1, k1 = nextP()
                MMG([(g1[:, 0:416], xnT[:, 128 * k:128 * (k + 1)], w_in_sb[:, k, 448:864], k == 0, k == 7) for k in range(8)],
                    ["xnT", "w_in_sb"], [k1])
                yield
                ACT(sq0, g0[:, 0:256], AF.Square, [k0], ["sqs"])
                RED(stat[:, 1:5], sq0.rearrange("p (g d) -> p g d", d=64), ["sqs"], ["stat_a"])
                ACT(sq1[:, 0:416], g1[:, 0:416], AF.Square, [k1], ["sqs"])
                RED(stat[:, 5:6], sq1[:, 0:256], ["sqs"], ["stat_b"])
                RED(stat[:, 6:7], sq1[:, 256:384], ["sqs"], ["stat_c"])
                RED(stat[:, 7:8], sq1[:, 384:416], ["sqs"], ["stat_d"])
                TT("dve", statm[:, 1:8], stat[:, 1:8], invn[:, 1:8], ALU.mult, ["stat_a", "stat_b", "stat_c", "stat_d", "invn"], ["statm"])
                ACT(lnv[:, 1:8], statm[:, 1:8], AF.Ln, ["statm"], ["lnv_r"], bias=EPSB[:, 0:1])
                ACT(rstd[:, 1:8], lnv[:, 1:8], AF.Exp, ["lnv_r"], ["rstd_r"], scale=-0.5)
                yield
                STT(qtok[:, 0:64], g0[:, 0:64], rstd[:, 1:2], gvb[:, G_AQ:G_AQ + 64], ALU.mult, ALU.mult, [k0, "rstd_r", "gvb"], ["qtok_a"])
                TSC("dve", qtok[:, 64:128], g0[:, 64:128], 0.125, None, ALU.mult, None, [k0], ["qtok_c"])
                STT(qtok[:, 128:192], g0[:, 128:192], rstd[:, 3:4], gvb[:, G_MQ:G_MQ + 64], ALU.mult, ALU.mult, [k0, "rstd_r", "gvb"], ["qtok_m"])
                STT(st[:, 0:64], g0[:, 192:256], rstd[:, 4:5], gvb[:, G_AK:G_AK + 64], ALU.mult, ALU.mult, [k0, "rstd_r", "gvb"], [(sk, "ak")])
                CP("dve", st[:, 64:256], g0[:, 256:448], [k0], [(sk, "cav")])
                yield
                CP("pool", kactok[:], st[:, 0:128], [(sk, "ak"), (sk, "cav")], ["kactok"])
                CP("pool", VA[:, blk, :], st[:, 128:192], [(sk, "cav")], [("VA", blk)])
                CP("pool", VC[:, blk, :], st[:, 192:256], [(sk, "cav")], [("VC", blk)])
                yield
                TSC("dve", cqn[:], g1[:, 0:256], rstd[:, 5:6], None, ALU.mult, None, [k1, "rstd_r"], ["cqn"])
                STT(st[:, 256:384], g1[:, 256:384], rstd[:, 6:7], gvb[:, G_CKV:G_CKV + 128], ALU.mult, ALU.mult, [k1, "rstd_r", "gvb"], [(sk, "ckv")])
                STT(krn[:], g1[:, 384:416], rstd[:, 7:8], gvb[:, G_BKR:G_BKR + 32], ALU.mult, ALU.mult, [k1, "rstd_r", "gvb"], ["krn"])
                rope(krn, cst, st[:, 384:416], ["krn", ck_], [(sk, "kr")])
                yield
                g2, k2 = nextP()
                MMG([(g2[:, 0:256], xnT[:, 128 * k:128 * (k + 1)], w_in_sb[:, k, 864:1120], k == 0, k == 7) for k in range(8)],
                    ["xnT", "w_in_sb"], [k2])
                yield
                ACT(ge[:], g2[:, 0:256], AF.Exp, [k2], ["ge"], scale=-1.0)
                ACT(ge[:], ge[:], AF.Ln, ["ge"], ["ge"], bias=1.0)
                ACT(ge[:], ge[:], AF.Exp, ["ge"], ["ge"], scale=-1.0)
                TT("dve", gtok[:], g2[:, 0:256], ge[:], ALU.mult, [k2, "ge"], ["gtok"])
                yield
                pt, pk = nextP()
                TRG([(bfv(pt)[:, 128 * j:128 * (j + 1)], cqn[:, 128 * j:128 * (j + 1)], ident_bf[:]) for j in range(2)], ["cqn", "ident_bf"], [pk])
                CP("dve", cqnT[:], bfv(pt)[:, 0:256], [pk], ["cqnT"])
                yield
                qb, kq = nextP()
                MMG([(qb[:, 0:96], cqnT[:, 128 * j:128 * (j + 1)], wq_sb[:, j, :], j == 0, j == 1) for j in range(2)], ["cqnT", "wq_sb"], [kq])
                ACT(sq2, qb[:, 0:96], AF.Square, [kq], ["sqs"])
                RED(stat2[:, 0:1], sq2[:, 0:64], ["sqs"], ["stat2a"])
                RED(stat2[:, 1:2], sq2[:, 64:96], ["sqs"], ["stat2b"])
                TT("dve", statm2[:, 0:2], stat2[:, 0:2], invn2[:, 0:2], ALU.mult, ["stat2a", "stat2b", "invn2"], ["statm2ab"])
                ACT(lnv2[:, 0:2], statm2[:, 0:2], AF.Ln, ["statm2ab"], ["lnv2ab"], bias=EPSB[:, 0:1])
                ACT(rstd2[:, 0:2], lnv2[:, 0:2], AF.Exp, ["lnv2ab"], ["rstd2ab"], scale=-0.5)
                yield
                STT(qbtok[:, 0:64], qb[:, 0:64], rstd2[:, 0:1], gvb[:, G_BQN:G_BQN + 64], ALU.mult, ALU.mult, [kq, "rstd2ab", "gvb"], ["qbtok_n"])
                STT(qrn[:], qb[:, 64:96], rstd2[:, 1:2], gvb[:, G_BQR:G_BQR + 32], ALU.mult, ALU.mult, [kq, "rstd2ab", "gvb"], ["qrn"])
                rope(qrn, cst, qbtok[:, 64:96], ["qrn", ck_], ["qbtok_r"])
                yield
                mla_kv(st[:, 256:384], st[:, 384:416], [(sk, "ckv"), (sk, "kr")], blk, c0)
                yield
                DMA(st_out[l, 128 * i:128 * (i + 1), :], st[:], [(sk, "ak"), (sk, "cav"), (sk, "ckv"), (sk, "kr")], [("st_out", l, i)])
                pa, ka = nextP()
                TRG([(bfv(pa)[:, 0:128], qtok[:, 0:128], ident_bf[:]), (bfv(pa)[0:64, 128:256], qtok[:, 128:192], ident_bf[:]),
                     (bfv(pa)[:, 256:384], kactok[:], ident_bf[:]), (bfv(pa)[0:96, 384:512], qbtok[:], ident_bf[:])],
                    ["qtok_a", "qtok_c", "qtok_m", "kactok", "qbtok_n", "qbtok_r", "ident_bf"], [ka])
                CP("dve", QTAC[sp_][:, 128 * u:128 * (u + 1)], bfv(pa)[:, 0:128], [ka], [("QTAC", sp_, u)])
                CP("dve", QTM[sp_][0:64, 128 * u:128 * (u + 1)], bfv(pa)[0:64, 128:256], [ka], [("QTM", sp_, u)])
                CP("dve", KTAC[:, c0:c0 + 128], bfv(pa)[:, 256:384], [ka], [("KTAC", blk)])
                CP("dve", QTB[sp_][0:96, 128 * u:128 * (u + 1)], bfv(pa)[0:96, 384:512], [ka], [("QTB", sp_, u)])
                yield
                pg, kg = nextP()
                TRG([(bfv(pg)[0:64, 128 * g:128 * (g + 1)], gtok[:, 64 * g:64 * (g + 1)], ident_bf[:]) for g in range(4)], ["gtok", "ident_bf"], [kg])
                CP("dve", GT[sp_][0:64, :, 128 * u:128 * (u + 1)], bfv(pg)[0:64, 0:512].rearrange("p (g c) -> p g c", g=4), [kg], [("GT", sp_, u)])

            pending = []
            pump_k = [1]

            def pump():
                for _ in range(pump_k[0]):
                    while pending:
                        try:
                            next(pending[0])
                            break
                        except StopIteration:
                            pending.pop(0)

            def drain():
                while pending:
                    try:
                        next(pending[0])
                    except StopIteration:
                        pending.pop(0)

            _sc = [0, 0, 0]

            def softmax_blocks(grp, sp_, qt, qrows, blocks, kt, vt, vkey, ktkey, first, last_flag):
                qkeys = [(qt[1], sp_, u) for u in range(4)]
                n = len(blocks)
                base_s = _sc[0]
                base_p = _sc[1]
                _sc[0] += n
                _sc[1] += n

                def S(b):
                    kc0, vb, c0, c1 = blocks[b][0:4]
                    si = (base_s + b) % 2
                    MMG([(psS[si][:, c0:c1], kt[qrows[0]:qrows[1], kc0:kc0 + 128], qt[0][sp_][qrows[0]:qrows[1], c0:c1], True, True)],
                        [(ktkey, kc0 // 128)] + qkeys, [("psS", si)])

                def E(b):
                    kc0, vb, c0, c1, mask, mkey, mc0, mc1 = blocks[b]
                    si = (base_s + b) % 2
                    pi = (base_p + b) % len(Pb)
                    ACT(Pb[pi][:, c0:c1], psS[si][:, c0:c1], AF.Exp, [("psS", si), "shift"], [("P", pi)], bias=shift[:, grp:grp + 1])
                    if mask is not None:
                        TT("pool", Pb[pi][:, mc0:mc1], Pb[pi][:, mc0:mc1], mask, ALU.mult, [("P", pi), mkey], [("P", pi)])

                def V(b):
                    kc0, vb, c0, c1 = blocks[b][0:4]
                    pi = (base_p + b) % len(Pb)
                    st_ = first and b == 0
                    sp2 = last_flag and b == n - 1
                    if vkey == "VM":
                        MMG([(psO[0:64, c0:c1], vt[:, vb, :], Pb[pi][:, c0:c1], st_, sp2)], [(vkey, vb), ("P", pi)], ["psO"])
                    else:
                        MMG([(psO[:, c0:c1], vt[:, vb:vb + 2, :].rearrange("p a b -> p (a b)"), Pb[pi][:, c0:c1], st_, sp2)],
                            [(vkey, vb), ("P", pi)], ["psO"])
                    if b == 0 and first:
                        CP("dve", Pacc[:, c0:c1], Pb[pi][:, c0:c1], [("P", pi)], ["Pacc"])
                    else:
                        TT("dve", Pacc[:, c0:c1], Pacc[:, c0:c1], Pb[pi][:, c0:c1], ALU.add, ["Pacc", ("P", pi)], ["Pacc"])
                    if sp2:
                        MMG([(psD[0:64, :], ones_f[:, 0:64], Pacc[:], True, True)], ["ones_f", "Pacc"], ["psD"])

                S(0)
                for b in range(n + 1):
                    if b < n:
                        E(b)
                    if b + 1 < n:
                        S(b + 1)
                    if b >= 1:
                        V(b - 1)
                    pump()

            def stick_blocks(sp_, blocks, first, last_flag):
                qkeys = [("QTAC", sp_, u) for u in range(4)]
                n = len(blocks)
                base = _sc[2]
                _sc[2] += n
                base_s = _sc[0]
                _sc[0] += n
                accb = [(psAcc, "psAcc"), (psD, "psD")]

                def ops(i):
                    kc0, vb, c0, c1 = blocks[i][0:4]
                    return KTAC[:, kc0:kc0 + 128], QTAC[sp_][:, c0:c1]

                def Z(i):
                    kc0, vb, c0, c1 = blocks[i][0:4]
                    si = (base_s + i) % 2
                    lhs, rhs = ops(i)
                    MMG([(psS[si][:, c0:c1], lhs, rhs, True, True)], [("KTAC", kc0 // 128)] + qkeys, [("psS", si)])

                def X1(i):
                    kc0, vb, c0, c1 = blocks[i][0:4]
                    si = (base_s + i) % 2
                    ei = (base + i) % 2
                    ACT(Eb[ei % NB1][:, c0:c1], psS[si][:, c0:c1], AF.Exp, [("psS", si)], [("E", ei % NB1)])

                def LN(i):
                    kc0, vb, c0, c1, mask, mkey, mc0, mc1 = blocks[i]
                    ei = (base + i) % 2
                    ACT(SPb[ei % NB1][:, c0:c1], Eb[ei % NB1][:, c0:c1], AF.Ln, [("E", ei % NB1)], [("SP", ei % NB1)], bias=1.0)
                    if mask is not None:
                        TT("dve", SPb[ei % NB1][:, mc0:mc1], SPb[ei % NB1][:, mc0:mc1].bitcast(F32), mask, ALU.mult, [("SP", ei % NB1), mkey], [("SP", ei % NB1)])

                def AC(i):
                    kc0, vb, c0, c1 = blocks[i][0:4]
                    ei = (base + i) % 2
                    pa, pak = accb[(base + i) % 2]
                    lhs, rhs = ops(i)
                    f0 = first and i == 0
                    mms = [(pa[:, c0:c1], lhs, rhs, True, False), (pa[:, c0:c1], negtri[:], SPb[ei % NB1][:, c0:c1], False, f0)]
                    rk = [("KTAC", kc0 // 128), ("SP", ei % NB1), "negtri"] + qkeys
                    if not f0:
                        mms.append((pa[:, c0:c1], negones[:], Rbb[ei][:, c0:c1], False, True))
                        rk += [("R", ei), "negones"]
                    MMG(mms, rk, [pak])

                def RU(i):
                    kc0, vb, c0, c1 = blocks[i][0:4]
                    ei = (base + i) % 2
                    TT("dve", Rbb[1 - ei][:, c0:c1], Rbb[ei][:, c0:c1].bitcast(F32), SPb[ei % NB1][:, c0:c1].bitcast(F32), ALU.add,
                       [("R", ei), ("SP", ei % NB1)], [("R", 1 - ei)])

                def X2(i):
                    kc0, vb, c0, c1, mask, mkey, mc0, mc1 = blocks[i]
                    ei = (base + i) % 2
                    pa, pak = accb[(base + i) % 2]
                    ACT(Wb[ei % NB1][:, c0:c1], pa[:, c0:c1], AF.Exp, [pak], [("W", ei % NB1)])
                    if mask is not None:
                        TT("pool", Wb[ei % NB1][:, mc0:mc1], Wb[ei % NB1][:, mc0:mc1], mask, ALU.mult, [("W", ei % NB1), mkey], [("W", ei % NB1)])

                def PV(i):
                    kc0, vb, c0, c1 = blocks[i][0:4]
                    ei = (base + i) % 2
                    MMG([(psO[:, c0:c1], VC[:, vb:vb + 2, :].rearrange("p a b -> p (a b)"), Wb[ei % NB1][:, c0:c1], first and i == 0, last_flag and i == n - 1)],
                        [("VC", vb), ("W", ei % NB1)], ["psO"])

                Z(0)
                if n > 1:
                    Z(1)
                X1(0)
                for i in range(n + 1):
                    if i < n:
                        LN(i)
                    if i + 2 < n:
                        Z(i + 2)
                    if i >= 2:
                        PV(i - 2)
                    if i + 1 < n:
                        X1(i + 1)
                    if i < n:
                        AC(i)
                        if i + 1 < n:
                            RU(i)
                    if i >= 1:
                        X2(i - 1)
                    pump()
                PV(n - 1)

            def zero_R():
                for i in range(2):
                    TSC("dve", Rbb[i][:], Rbb[i][:].bitcast(F32), 0.0, None, ALU.mult, None, [("R", i)], [("R", i)])

            def post_group(grp, sp_, t, softmax, ybi):
                cols = slice(512 * t, 512 * (t + 1))
                ACT(psq[:], psO[0:64, :], AF.Square, ["psO"], ["psq"])
                si_ = _sc[0] % 2
                _sc[0] += 1
                pt, pk = psS[si_], ("psS", si_)
                MMG([(pt[0:64, :], ones_f[0:64, 0:64], psq[:], True, True)], ["ones_f", "psq"], [pk])
                if softmax:
                    CP("dve", pd[:], psD[0:64, :], ["psD"], ["pd"])
                    STT(pd[:], pd[:], EPS, pd[:], ALU.mult, ALU.mult, ["pd"], ["pd"])
                    STT(pd[:], pt[0:64, :], 1.0 / 64, pd[:], ALU.mult, ALU.add, [pk, "pd"], ["pd"])
                else:
                    TSC("dve", pd[:], pt[0:64, :], 1.0 / 64, EPS, ALU.mult, ALU.add, [pk], ["pd"])
                ACT(pd[:], pd[:], AF.Ln, ["pd"], ["pd"])
                ACT(pd[:], pd[:], AF.Exp, ["pd"], ["pd"], scale=-0.5)
                STT(psq[:], psO[0:64, :], outg_sb[0:64, grp:grp + 1], pd[:], ALU.mult, ALU.mult, ["psO", "outg", "pd"], ["psq"])
                yt = yTb[ybi]
                TT("pool", yt[:], psq[:], GT[sp_][0:64, grp, :], ALU.mult, ["psq"] + [("GT", sp_, u) for u in range(4)], [("yT", ybi)])
                yops.append(DMA(yT_loc[256 * t + 64 * grp:256 * t + 64 * (grp + 1), :], yt[:], [("yT", ybi)], [("yT_loc", grp, t)]))

            _yb = [0]

            def ybuf():
                _yb[0] += 1
                return _yb[0] % len(yTb)

            def attention_prompt(t, sp_):
                nb = 4 * t + 4
                blocks = []
                for r in range(-4, 4):
                    j = 4 * t + r
                    if j < 0:
                        continue
                    blocks.append((128 * j, j, 0, 512, MA[r + 4][:], ("MA", r + 4), 0, 512))
                softmax_blocks(0, sp_, (QTAC, "QTAC"), (0, 64), blocks, KTAC, VA, "VA", "KTAC", True, True)
                post_group(0, sp_, t, True, ybuf())
                blocks = [(128 * j, j, 0, 512, None, None, 0, 0) for j in range(4 * t)]
                for r in range(4):
                    j = 4 * t + r
                    blocks.append((128 * j, j, 128 * r, 512, BLK[:], "BLK", 128 * r, 128 * r + 128))
                softmax_blocks(1, sp_, (QTB, "QTB"), (0, 128), blocks, KTB, VB, "VB", "KTB", True, True)
                post_group(1, sp_, t, True, ybuf())
                MS("pool", QTAC[sp_][0:64, :], 0.0, [("QTAC", sp_, u) for u in range(4)])
                zero_R()
                blocks = []
                for r in range(3, -1, -1):
                    j = 4 * t + r
                    blocks.append((128 * j, j, 128 * r, 512, TRI[:], "TRI", 128 * r, 128 * r + 128))
                for j in range(4 * t - 1, -1, -1):
                    blocks.append((128 * j, j, 0, 512, None, None, 0, 0))
                stick_blocks(sp_, blocks, True, True)
                post_group(2, sp_, t, False, ybuf())
                blocks = [(128 * j, j, 0, 512, None, None, 0, 0) for j in range(2)]
                softmax_blocks(3, sp_, (QTM, "QTM"), (0, 64), blocks, KTM, VM, "VM", "KTM", True, True)
                post_group(3, sp_, t, True, ybuf())

            def attention_sample(t, sp_):
                sq_base = (t - NPS) * 8
                for s8 in range(8):
                    s = sq_base + s8
                    c0, c1 = 64 * s8, 64 * s8 + 64
                    hf = s % 2
                    nblk_i = NPT + s // 2
                    blocks = [(128 * (CB0 + 8 * s + i), CB0 + 8 * s + i, c0, c1, MA[i][:, 0:64], ("MA", i), c0, c1) for i in range(4)]
                    blocks.append((128 * nblk_i, nblk_i, c0, c1, (MA[4][:, 0:64] if hf == 0 else MAs1[:]),
                                   (("MA", 4) if hf == 0 else "MAs1"), c0, c1))
                    softmax_blocks(0, sp_, (QTAC, "QTAC"), (0, 64), blocks, KTAC, VA, "VA", "KTAC", True, s8 == 7)
                post_group(0, sp_, t, True, ybuf())
                for s8 in range(8):
                    s = sq_base + s8
                    c0, c1 = 64 * s8, 64 * s8 + 64
                    hf = s % 2
                    nblk_i = NPT + s // 2
                    blocks = [(128 * (CB0 + 8 * s + i), CB0 + 8 * s + i, c0, c1, None, None, 0, 0) for i in range(8)]
                    blocks.append((128 * nblk_i, nblk_i, c0, c1, HS[hf][:], f"HS{hf}", c0, c1))
                    softmax_blocks(1, sp_, (QTB, "QTB"), (0, 128), blocks, KTB, VB, "VB", "KTB", True, s8 == 7)
                post_group(1, sp_, t, True, ybuf())
                MS("pool", QTAC[sp_][0:64, :], 0.0, [("QTAC", sp_, u) for u in range(4)])
                zero_R()
                for s8 in range(8):
                    s = sq_base + s8
                    c0, c1 = 64 * s8, 64 * s8 + 64
                    hf = s % 2
                    nblk_i = NPT + s // 2
                    blocks = [(128 * nblk_i, nblk_i, c0, c1, (TRI[:, 0:64] if hf == 0 else MCs1[:]), ("TRI" if hf == 0 else "MCs1"), c0, c1)]
                    blocks += [(128 * (CB0 + 8 * s + i), CB0 + 8 * s + i, c0, c1, None, None, 0, 0) for i in range(7, -1, -1)]
                    stick_blocks(sp_, blocks, True, s8 == 7)
                post_group(2, sp_, t, False, ybuf())
                for s8 in range(8):
                    s = sq_base + s8
                    c0, c1 = 64 * s8, 64 * s8 + 64
                    blocks = [(128 * (MB0 + 2 * s8 + i), MB0 + 2 * s8 + i, c0, c1, None, None, 0, 0) for i in range(2)]
                    softmax_blocks(3, sp_, (QTM, "QTM"), (0, 64), blocks, KTM, VM, "VM", "KTM", True, s8 == 7)
                post_group(3, sp_, t, True, ybuf())

            def cache_prep(s_lo, s_hi, do_main=True, do_m=True):
                r2 = lambda ap: ap.rearrange("(i p) d -> p i d", p=128)
                for s in range(s_lo, s_hi):
                    for c4 in (range(4) if do_main else ()):
                        rows = slice(256 * c4, 256 * c4 + 256)
                        if c4 < 2:
                            DMA(kcst[:, :, 0:64], r2(c_ak[l, s, rows, :]), [], ["kcst_a"])
                            DMA(vast[:], r2(c_av[l, s, rows, :]), [], ["vast"])
                        DMA(kcst[:, :, 64:128], r2(c_ck[l, s, rows, :]), [], ["kcst_c"])
                        DMA(vcst[:], r2(c_cv[l, s, rows, :]), [], ["vcst"])
                        DMA(ckvst[:], r2(c_ckv[l, s, rows, :]), [], ["ckvst"])
                        DMA(krst[:], r2(c_kr[l, s, rows, :]), [], ["krst"])
                        for i2 in range(2):
                            i = 2 * c4 + i2
                            blk = CB0 + 8 * s + i
                            pt, pk = nextP()
                            P.add("pe", (lambda e, o=pt[:, 0:128], a_=kcst[:, i2, :]: e.transpose(out=o, in_=a_, identity=ident_f[:])),
                                  ["kcst_a", "kcst_c", "ident_f"], [pk])
                            CP("dve", KTAC[:, 128 * blk:128 * blk + 128], pt[:, 0:128], [pk], [("KTAC", blk)])
                            if i < 4:
                                CP("pool", VA[:, blk, :], vast[:, i2, :], ["vast"], [("VA", blk)])
                            CP("pool", VC[:, blk, :], vcst[:, i2, :], ["vcst"], [("VC", blk)])
                            mla_kv(ckvst[:, i2, :], krst[:, i2, :], ["ckvst", "krst"], blk, 128 * blk)
                            yield
                    if not do_m:
                        continue
                    DMA(mkst[:], r2(c_mk[l, s]), [], ["mkst"])
                    DMA(mvst[:], r2(c_mv[l, s]), [], ["mvst"])
                    for i in range(2):
                        pt, pk = nextP()
                        P.add("pe", (lambda e, o=pt[0:64, 0:128], a_=mkst[:, i, :]: e.transpose(out=o, in_=a_, identity=ident_f[:])),
                              ["mkst", "ident_f"], [pk])
                        mb = MB0 + 2 * (s % 8) + i
                        CP("dve", KTM[0:64, 128 * mb:128 * mb + 128], pt[0:64, 0:128], [pk], [("KTM", mb)])
                        CP("pool", VM[:, mb, :], mvst[:, i, :], ["mvst"], [("VM", mb)])

            yops = []
            ag_ops = []
            load_tile(0)
            for u in range(4):
                pending.append(project_tile(u, 0, u))
            drain()
            for t in range(NSUP):
                sp_ = t % NBUF
                if t == NPS:
                    for _ in cache_prep(0, 8):
                        pass
                if t == NPS + 1:
                    for _ in cache_prep(8, 16, do_main=(NBUF == 1), do_m=True):
                        pass
                if t == NPS and NBUF == 2:
                    pending.append(cache_prep(8, 16, do_main=True, do_m=False))
                if t + 1 < NSUP and NBUF == 2:
                    for u in range(4):
                        pending.append(project_tile(4 * (t + 1) + u, (t + 1) % NBUF, u))
                    n_it = (min(8, 4 * t + 4) + 2 * (4 * t + 4) + 2 + 4) if t < NPS else 216
                    pump_k[0] = max(1, -(-60 // n_it))
                del yops[:]
                if t < NPS:
                    attention_prompt(t, sp_)
                else:
                    attention_sample(t, sp_)
                ag_t = Op("pool", None, True)
                ag_t.inc = 1
                ag_t.sem = ccsem
                ag_t.deps = set(yops)
                ag_t.fn = (lambda e, t=t: e.collective_compute(
                    "AllGather", ALU.bypass, replica_groups=[[0, 1, 2, 3], [4, 5, 6, 7]],
                    ins=[yT_loc[256 * t:256 * (t + 1), :]], outs=[yT_all[1024 * t:1024 * (t + 1), :]]))
                P.ops["pool"].append(ag_t)
                P.order.append(ag_t)
                ag_ops.append(ag_t)
                drain()
                if t + 1 < NSUP and NBUF == 1:
                    for u in range(4):
                        pending.append(project_tile(4 * (t + 1) + u, 0, u))
                    drain()

            ag = ag_ops[-1]

            prev_ag = ag
            for k in range(8):
                w_ = XT[k % 2]
                wk = ("xt", k % 2)
                DMA(w_[:, 0:1024], wout_d[l, 128 * k:128 * (k + 1), :], [], [wk], extra=[ag])
                TSC("dve", wout_sb[:, k, :], w_[:, 0:1024], 1.0, None, ALU.mult, None, [wk], ["wout_sb"])
            bankrot = [(psS[0], ("psS", 0)), (psS[1], ("psS", 1)), (psAcc, "psAcc"), (psO, "psO"), (psD, "psD")]
            bi_ = 0
            for t in range(NSUP):
                yt_ = YTs[t % 2]
                DMA(yt_, yT_all[1024 * t:1024 * (t + 1), :].rearrange("(k p) c -> p k c", p=128), [], [("YT", t % 2)], extra=[ag_ops[t], ag])
                for u in range(4):
                    i = 4 * t + u
                    xo = XO[i % 2]
                    DMA(xo[:], xsrc[128 * i:128 * (i + 1), :], [("xdram", i)], [("xt", i % 2)])
                    for half in range(2):
                        pb, pbk = bankrot[bi_ % 5]
                        bi_ += 1
                        MMG([(pb[:, :], yt_[:, k, 128 * u:128 * (u + 1)], wout_sb[:, k, 512 * half:512 * (half + 1)], k == 0, k == 7)
                             for k in range(8)], [("YT", t % 2), "wout_sb"], [pbk])
                        TT("dve", xo[:, 512 * half:512 * (half + 1)], pb[:, :], xo[:, 512 * half:512 * (half + 1)], ALU.add,
                           [pbk, ("xt", i % 2)], [("xt", i % 2)])
                    if l < L - 1:
                        DMA(xdst[128 * i:128 * (i + 1), :], xo[:], [("xt", i % 2)], [("xdram", i)])
                    else:
                        DMA(xdst[128 * i:128 * (i + 1), :], xo[:], [("xt", i % 2)], [("yout", i)])

        P.finalize(esem, dsems)
        with nc.Block() as block:
            @block.sync
            def _(e):
                P.emit("sp", e)

            @block.scalar
            def _(e):
                P.emit("act", e)

            @block.vector
            def _(e):
                P.emit("dve", e)

            @block.gpsimd
            def _(e):
                P.emit("pool", e)

            @block.tensor
            def _(e):
                P.emit("pe", e)
        print("ops", len(P.order), {e: len(v) for e, v in P.ops.items()}, flush=True)
    return nc


IN_OFF = dict(aq=0, ak=256, av=512, ag=768, bcq=1024, bckv=1280, bkr=1408, bg=1440, cq=1696, ck=1952, cv=2208, cg=2464,
              mq=2720, mg=2976)


def _cols(h):
    def hd(name):
        o = IN_OFF[name] + 64 * h
        return list(range(o, o + 64))
    c = hd("aq") + hd("cq") + hd("mq") + hd("ak") + hd("ck") + hd("av") + hd("cv")
    c += list(range(1024, 1280)) + list(range(1280, 1408)) + list(range(1408, 1440))
    c += hd("ag") + hd("bg") + hd("cg") + hd("mg")
    return np.asarray(c)


def _rope_table(S):
    half = 16
    freqs = (np.float32(10000.0) ** (-np.arange(half, dtype=np.float32) / np.float32(half))).astype(np.float32)
    pos = np.concatenate([np.arange(S), np.tile(1024 + np.arange(TS), NSEQ)]).astype(np.float32)
    ang = (pos[:, None] * freqs[None, :]).astype(np.float32)
    c, s = np.cos(ang).astype(np.float32), np.sin(ang).astype(np.float32)
    return np.ascontiguousarray(np.concatenate([c, c, -s, s], 1))


def make_in_maps(inp, S, L):
    f = lambda a: np.ascontiguousarray(np.asarray(a, dtype=np.float32))
    cs = _rope_table(S)
    maps = []
    perm = np.asarray([g * 256 + r * 64 + d for r in range(4) for g in range(4) for d in range(64)])
    wout = f(np.asarray(inp["w_out"])[:L][:, perm, :])
    for c in range(8):
        g, h = c // 4, c % 4
        sl = slice(NSEQ * g, NSEQ * g + NSEQ)
        cols = _cols(h)
        gv = np.concatenate([inp["a_qn_g"][:L], inp["m_qn_g"][:L], inp["a_kn_g"][:L], inp["b_qn_g"][:L], inp["b_qr_g"][:L],
                             inp["b_kn_g"][:L], inp["b_kr_g"][:L], inp["b_ckv_g"][:L], inp["m_kn_g"][:L]], axis=1)
        og = np.asarray(inp["out_g"])[:L].reshape(L, 4, 4, 64)[:, :, h, :].transpose(0, 2, 1)
        wm = np.asarray(inp["w_mem_kv"])[:L]
        m = dict(
            x=f(np.concatenate([np.asarray(inp["x_prompt"])[g, :S], np.asarray(inp["x_sample"])[sl].reshape(NSEQ * TS, D)], 0)),
            mem=f(inp["mem_prompt"][g]),
            cs=cs,
            w_in=f(np.asarray(inp["w_in"])[:L][:, :, cols]),
            ng=f(np.asarray(inp["norm_g"])[:L].reshape(L, 8, 128).transpose(0, 2, 1)),
            wq=f(np.asarray(inp["b_wq_b"])[:L][:, :, 96 * h:96 * h + 96]),
            cqg=f(np.asarray(inp["b_cq_g"])[:L].reshape(L, 2, 128).transpose(0, 2, 1)),
            wkv=f(np.asarray(inp["b_wkv_b"])[:L][:, :, 128 * h:128 * h + 128]),
            wmem=f(np.concatenate([wm[:, :, 64 * h:64 * h + 64], wm[:, :, 256 + 64 * h:256 + 64 * h + 64]], 2)),
            mng=f(np.asarray(inp["m_norm_g"])[:L].reshape(L, 8, 128).transpose(0, 2, 1)),
            wout=wout,
            gv=f(gv),
            outg=f(og),
            tab=f(np.asarray(inp["a_rel_bias"])[:L][:, h, :]),
            c_ak=f(np.asarray(inp["cache_a_k"])[:L][:, sl, :, h, :]),
            c_av=f(np.asarray(inp["cache_a_v"])[:L][:, sl, :, h, :]),
            c_ck=f(np.asarray(inp["cache_c_k"])[:L][:, sl, :, h, :]),
            c_cv=f(np.asarray(inp["cache_c_v"])[:L][:, sl, :, h, :]),
            c_ckv=f(np.asarray(inp["cache_b_ckv"])[:L][:, sl]),
            c_kr=f(np.asarray(inp["cache_b_krope"])[:L][:, sl]),
            c_mk=f(np.asarray(inp["cache_mem_k"])[:L][:, sl, :, h, :]),
            c_mv=f(np.asarray(inp["cache_mem_v"])[:L][:, sl, :, h, :]),
        )
        maps.append(m)
    return maps


def assemble(res, S, L):
    B = 2
    DB = 2 * NSEQ
    o = lambda c, n: np.asarray(res[c][n])
    y_p = np.stack([o(4 * g, "y")[:S] for g in range(B)])
    y_s = np.concatenate([o(4 * g, "y")[S:].reshape(NSEQ, TS, D) for g in range(B)], 0)
    keep = min(512, S)
    a_k_p = np.zeros((L, B, keep, 4, 64), np.float32)
    a_v_p = np.zeros_like(a_k_p)
    ckv_p = np.zeros((L, B, S, 128), np.float32)
    kr_p = np.zeros((L, B, S, 32), np.float32)
    c_k_p = np.zeros((L, B, S, 4, 64), np.float32)
    c_v_p = np.zeros_like(c_k_p)
    m_k_p = np.zeros((L, B, 256, 4, 64), np.float32)
    m_v_p = np.zeros_like(m_k_p)
    a_k_s = np.zeros((L, DB, 512, 4, 64), np.float32)
    a_v_s = np.zeros_like(a_k_s)
    ckv_s = np.zeros((L, DB, TS, 128), np.float32)
    kr_s = np.zeros((L, DB, TS, 32), np.float32)
    c_k_s = np.zeros((L, DB, TS, 4, 64), np.float32)
    c_v_s = np.zeros_like(c_k_s)
    for c in range(8):
        g, h = c // 4, c % 4
        st = o(c, "o_state")
        sp = st[:, :S]
        ss = st[:, S:].reshape(L, NSEQ, TS, NST)
        sl = slice(NSEQ * g, NSEQ * g + NSEQ)
        a_k_p[:, g, :, h, :] = sp[:, S - keep:, 0:64]
        a_v_p[:, g, :, h, :] = sp[:, S - keep:, 128:192]
        c_k_p[:, g, :, h, :] = sp[:, :, 64:128]
        c_v_p[:, g, :, h, :] = sp[:, :, 192:256]
        mm = o(c, "o_mem")
        m_k_p[:, g, :, h, :] = mm[:, :, 0:64]
        m_v_p[:, g, :, h, :] = mm[:, :, 64:128]
        a_k_s[:, sl, 0:448, h, :] = o(c, "o_aks")
        a_v_s[:, sl, 0:448, h, :] = o(c, "o_avs")
        a_k_s[:, sl, 448:512, h, :] = ss[:, :, :, 0:64]
        a_v_s[:, sl, 448:512, h, :] = ss[:, :, :, 128:192]
        c_k_s[:, sl, :, h, :] = ss[:, :, :, 64:128]
        c_v_s[:, sl, :, h, :] = ss[:, :, :, 192:256]
        if h == 0:
            ckv_p[:, g] = sp[:, :, 256:384]
            kr_p[:, g] = sp[:, :, 384:416]
            ckv_s[:, sl] = ss[:, :, :, 256:384]
            kr_s[:, sl] = ss[:, :, :, 384:416]
    return (y_p, y_s, a_k_p, a_v_p, ckv_p, kr_p, c_k_p, c_v_p, m_k_p, m_v_p, a_k_s, a_v_s, ckv_s, kr_s, c_k_s, c_v_s)


def run(inp, S, L):
    nc = build(S, L)
    maps = make_in_maps(inp, S, L)
    res = run_bass_kernel_spmd(nc, maps, core_ids=list(range(8)))
    return assemble(res.results, S, L)


def kernel(**inputs):
    inp = {k: np.asarray(v) for k, v in inputs.items()}
    return run(inp, 16384, 2)
```

```python
import os
from contextlib import ExitStack
import numpy as np
import concourse.bass as bass
import concourse.mybir as mybir
from concourse.bass_utils import run_bass_kernel_spmd

F32 = mybir.dt.float32
F32R = mybir.dt.float32r
BF16 = mybir.dt.bfloat16
AF = mybir.ActivationFunctionType
ALU = mybir.AluOpType
AX = mybir.AxisListType

D = 1024
NSEQ = 16
TS = 64
EPS = 1e-6
MLA_SCALE = 96.0 ** -0.5
NCOL = 1120
NST = 416
GV = 576
G_AQ, G_MQ, G_AK, G_BQN, G_BQR, G_BKN, G_BKR, G_CKV, G_MK = 0, 64, 128, 192, 256, 288, 352, 384, 512
SAME_ENGINE_SYNC = os.environ.get("K_SES", "1") == "1"
NDS = 24


class Op:
    __slots__ = ("eng", "fn", "deps", "sig", "sem", "val", "dma", "needs", "inc")

    def __init__(self, eng, fn, dma):
        self.eng, self.fn, self.dma = eng, fn, dma
        self.deps = set()
        self.sig = None
        self.sem = None
        self.val = None
        self.needs = False
        self.inc = 16 if dma else 1


class Prog:
    ENGS = ("sp", "act", "dve", "pool", "pe")

    def __init__(self):
        self.ops = {e: [] for e in self.ENGS}
        self.order = []
        self.lw = {}
        self.rd = {}

    def add(self, eng, fn, r=(), w=(), dma=False, extra=()):
        op = Op(eng, fn, dma)
        deps = set(extra)
        for b in r:
            o = self.lw.get(b)
            if o is not None:
                deps.add(o)
        for b in w:
            o = self.lw.get(b)
            if o is not None:
                deps.add(o)
            rr = self.rd.get(b)
            if rr:
                deps.update(rr.values())
        for b in w:
            self.lw[b] = op
            self.rd[b] = {}
        for b in r:
            d = self.rd.setdefault(b, {})
            d[id(op) if dma else eng] = op
        deps.discard(op)
        op.deps = deps
        self.ops[eng].append(op)
        self.order.append(op)
        return op

    def finalize(self, esem, dsems):
        last_on = {}
        cnt = 0
        tot = {}
        for op in self.order:
            if op.dma and op.sem is None:
                s = dsems[cnt % len(dsems)]
                cnt += 1
                if s in last_on:
                    op.deps.add(last_on[s])
                last_on[s] = op
                tot[s] = tot.get(s, 0) + op.inc
                op.sem, op.val = s, tot[s]
            elif op.dma:
                tot[op.sem] = tot.get(op.sem, 0) + op.inc
                op.val = tot[op.sem]
        for op in self.order:
            for d in op.deps:
                if not d.dma:
                    if d.eng == op.eng and (d.eng == "pe" or not SAME_ENGINE_SYNC):
                        continue
                    d.needs = True
        for e in self.ENGS:
            n = 0
            for op in self.ops[e]:
                if not op.dma and op.needs:
                    n += 1
                    op.sig = n
                    op.sem = esem[e]
        self.final = dict(tot)

    def emit(self, ename, eng):
        waited = {}
        for op in self.ops[ename]:
            need = {}
            for d in op.deps:
                if d.dma:
                    key, v = d.sem, d.val
                else:
                    if d.sig is None:
                        continue
                    key, v = d.sem, d.sig
                if waited.get(key, 0) >= v:
                    continue
                if need.get(key, 0) < v:
                    need[key] = v
            for key, v in need.items():
                eng.wait_ge(key, v)
                waited[key] = v
            inst = op.fn(eng)
            if op.dma and op.inc == 1:
                inst.then_inc(op.sem)
            elif op.dma:
                inst.then_inc(op.sem, op.inc)
            elif op.needs:
                inst.then_inc(op.sem, 1)
        if ename == "sp":
            for s, v in self.final.items():
                if waited.get(s, 0) < v:
                    eng.wait_ge(s, v)


def build(S=16384, L=2, debug=False):
    T = S + NSEQ * TS
    NT = T // 128
    NSUP = T // 512
    NPT = S // 128
    NPS = S // 512
    CB0 = 0 if NPT >= 128 else NT
    NB = max(NT, CB0 + 128)
    MB0 = 2

    nc = bass.Bass("TRN2", target_bir_lowering=False, dynamic_dma_scratch_size=1024)
    es = ExitStack()
    with es:
        try:
            es.enter_context(nc.allow_low_precision("bf16 matmul operands, fp32 accumulation"))
        except Exception:
            pass
        try:
            es.enter_context(nc.allow_non_contiguous_dma("toeplitz / strided layouts"))
        except Exception:
            pass

        def din(name, shape, dt=F32):
            return nc.dram_tensor(name, list(shape), dt, kind="ExternalInput").ap()

        def dout(name, shape, dt=F32):
            return nc.dram_tensor(name, list(shape), dt, kind="ExternalOutput").ap()

        x_in = din("x", [T, D])
        mem_in = din("mem", [256, D])
        cs_in = din("cs", [T, 64])
        w_in_d = din("w_in", [L, D, NCOL])
        ng_d = din("ng", [L, 128, 8])
        wq_d = din("wq", [L, 256, 96])
        cqg_d = din("cqg", [L, 128, 2])
        wkv_d = din("wkv", [L, 128, 128])
        wmem_d = din("wmem", [L, D, 128])
        mng_d = din("mng", [L, 128, 8])
        wout_d = din("wout", [L, D, D])
        gv_d = din("gv", [L, GV])
        outg_d = din("outg", [L, 64, 4])
        tab_d = din("tab", [L, 513])
        c_ak = din("c_ak", [L, NSEQ, 512, 64])
        c_av = din("c_av", [L, NSEQ, 512, 64])
        c_ck = din("c_ck", [L, NSEQ, 1024, 64])
        c_cv = din("c_cv", [L, NSEQ, 1024, 64])
        c_ckv = din("c_ckv", [L, NSEQ, 1024, 128])
        c_kr = din("c_kr", [L, NSEQ, 1024, 32])
        c_mk = din("c_mk", [L, NSEQ, 256, 64])
        c_mv = din("c_mv", [L, NSEQ, 256, 64])

        y_out = dout("y", [T, D])
        st_out = dout("o_state", [L, T, NST])
        mem_out = dout("o_mem", [L, 256, 128])
        aks_out = dout("o_aks", [L, NSEQ, 448, 64])
        avs_out = dout("o_avs", [L, NSEQ, 448, 64])

        xmid = nc.dram_tensor("xmid", [T, D], F32).ap()
        yT_loc = nc.dram_tensor("yT_loc", [NSUP * 256, 512], BF16).ap()
        yT_all = nc.dram_tensor("yT_all", [NSUP * 1024, 512], BF16).ap()
        ext_d = nc.dram_tensor("ext_d", [L, 2048], F32).ap()
        zt_d = nc.dram_tensor("zt_d", [128, 1921], F32).ap()

        def sb(name, shape, dt):
            return es.enter_context(nc.sbuf_tensor(name, list(shape), dt))

        def ps(name):
            return es.enter_context(nc.psum_tensor(name, [128, 512], F32))

        KTAC = sb("KTAC", [128, NB * 128], BF16)
        KTB = sb("KTB", [128, NB * 128], BF16)
        VA = sb("VA", [128, NB + 1, 64], BF16)
        VB = sb("VB", [128, NB + 1, 64], BF16)
        VC = sb("VC", [128, NB + 1, 64], BF16)
        KTM = sb("KTM", [64, 18 * 128], BF16)
        VM = sb("VM", [128, 18, 64], BF16)
        w_in_sb = sb("w_in_sb", [128, 8, NCOL], BF16)
        wmem_sb = sb("wmem_sb", [128, 8, 128], BF16)
        wq_sb = sb("wq_sb", [128, 2, 96], BF16)
        wkv_sb = sb("wkv_sb", [128, 128], BF16)
        wout_sb = KTB[:, 0:8192].rearrange("p (k c) -> p k c", k=8)
        YTs = [KTB[:, 8192 + i * 4096: 8192 + (i + 1) * 4096].rearrange("p (k c) -> p k c", k=8) for i in range(2)]
        MA = [sb(f"MA{i}", [128, 512], BF16) for i in range(8)]
        MAs1 = sb("MAs1", [128, 64], BF16)
        MCs1 = sb("MCs1", [128, 64], BF16)
        HS = [sb(f"HS{i}", [128, 64], BF16) for i in range(2)]
        TRI = sb("TRI", [128, 128], BF16)
        BLK = sb("BLK", [128, 128], BF16)
        negtri = sb("negtri", [128, 128], F32R)
        negones = sb("negones", [128, 128], F32R)
        ones_bf = sb("ones_bf", [128, 128], BF16)
        ones_f = sb("ones_f", [128, 128], F32)
        ident_bf = sb("ident_bf", [128, 128], BF16)
        ident_f = sb("ident_f", [128, 128], F32)
        invn = sb("invn", [128, 8], F32)
        invn2 = sb("invn2", [128, 4], F32)
        shift = sb("shift", [128, 4], F32)
        gvb = sb("gvb", [128, GV], F32)
        outg_sb = sb("outg_sb", [64, 4], F32)
        ngs = sb("ngs", [128, 8], F32)
        mngs = sb("mngs", [128, 8], F32)
        cqgs = sb("cqgs", [128, 2], F32)
        exrow = KTAC[0:1, 0:4096].bitcast(F32)
        zrow = KTAC[0:1, 4096:5632].bitcast(F32)
        exkeys = [("KTAC", j) for j in range(44)]

        XT = [sb(f"xt{i}", [128, D], F32) for i in range(2)]
        CS = [sb(f"cs{i}", [128, 64], F32) for i in range(2)]
        xn = sb("xn", [128, D], BF16)
        junk = xn
        xnT = sb("xnT", [128, D], BF16)
        ST = [sb(f"st{i}", [128, NST], F32) for i in range(1)]
        stat = sb("stat", [128, 8], F32)
        statm = sb("statm", [128, 8], F32)
        lnv = sb("lnv", [128, 8], F32)
        rstd = sb("rstd", [128, 8], F32)
        stat2 = sb("stat2", [128, 4], F32)
        statm2 = sb("statm2", [128, 4], F32)
        lnv2 = sb("lnv2", [128, 4], F32)
        rstd2 = sb("rstd2", [128, 4], F32)
        sqs = sb("sqs", [128, 416], F32)
        wsm = [sqs[:, 96 * i:96 * (i + 1)] for i in range(2)]
        tmpf = sqs[:, 192:320]
        sq0 = sqs[:, 0:256]
        sq1 = sqs
        sq2 = sqs[:, 0:96]
        qtok = sb("qtok", [128, 192], BF16)
        kactok = sb("kactok", [128, 128], BF16)
        cqn = sb("cqn", [128, 256], BF16)
        cqnT = sb("cqnT", [128, 256], BF16)
        ckvb = sb("ckvb", [128, 128], BF16)
        ckvT = sb("ckvT", [128, 128], BF16)
        krn = sb("krn", [128, 32], F32)
        qrn = sb("qrn", [128, 32], F32)
        rpa = sb("rpa", [128, 32], F32)
        rpb = sb("rpb", [128, 32], F32)
        kbtok = sb("kbtok", [128, 96], BF16)
        qbtok = sb("qbtok", [128, 96], BF16)
        ge = sb("ge", [128, 256], F32)
        gtok = sb("gtok", [128, 256], BF16)
        kmtok = sb("kmtok", [128, 64], BF16)
        stm = sb("stm", [128, 128], F32)

        NBUF = 2 if NPT >= 128 else int(os.environ.get("K_NBUF", "1"))
        NB1 = 2 if NPT >= 128 else 1
        QTAC = [sb(f"QTAC{i}", [128, 512], BF16) for i in range(NBUF)]
        QTB = [sb(f"QTB{i}", [128, 512], BF16) for i in range(NBUF)]
        QTM = [sb(f"QTM{i}", [64, 512], BF16) for i in range(NBUF)]
        GT = [sb(f"GT{i}", [64, 4, 512], BF16) for i in range(NBUF)]
        Eb = [sb(f"E{i}", [128, 512], F32) for i in range(NB1)]
        SPb = [sb(f"SP{i}", [128, 512], F32R) for i in range(NB1)]
        Rbb = [sb(f"R{i}", [128, 512], F32R) for i in range(2)]
        Wb = [sb(f"W{i}", [128, 512], BF16) for i in range(NB1)]
        Pb = [sb(f"P{i}", [128, 512], BF16) for i in range(2)]
        psq = sb("psq", [64, 512], F32)
        pd = sb("pd", [64, 512], F32)
        yTb = [sb(f"yT{i}", [64, 512], BF16) for i in range(1)]
        Pacc = sb("Pacc", [128, 512], F32)
        XO = XT
        kcst = sb("kcst", [128, 2, 128], F32)
        vast = sb("vast", [128, 2, 64], F32)
        vcst = sb("vcst", [128, 2, 64], F32)
        ckvst = sb("ckvst", [128, 2, 128], F32)
        krst = sb("krst", [128, 2, 32], F32)
        mkst = sb("mkst", [128, 2, 64], F32)
        mvst = sb("mvst", [128, 2, 64], F32)
        tz = Eb[0]
        vis = Wb[0]

        psP = [ps(f"psP{i}") for i in range(3)]
        psS = [ps(f"psS{i}") for i in range(2)]
        psAcc = ps("psAcc")
        psO = ps("psO")
        psD = ps("psD")

        esem = {e: es.enter_context(nc.semaphore(f"sem_{e}")) for e in Prog.ENGS}
        dsems = [es.enter_context(nc.semaphore(f"dsem{i}")) for i in range(NDS)]
        ccsem = es.enter_context(nc.semaphore("ccsem"))

        P = Prog()
        try:
            print("sbuf bytes remaining", nc.sbuf_bytes_remaining, flush=True)
        except Exception as ex:
            print("sbuf remaining n/a", ex)

        def ACT(out, in_, func, r, w, **kw):
            return P.add("act", lambda e: e.activation(out=out, in_=in_, func=func, **kw), r, w)

        def TSC(eng, out, in0, s1, s2, op0, op1, r, w):
            if op1 is None:
                return P.add(eng, lambda e: e.tensor_scalar(out=out, in0=in0, scalar1=s1, scalar2=None, op0=op0), r, w)
            return P.add(eng, lambda e: e.tensor_scalar(out=out, in0=in0, scalar1=s1, scalar2=s2, op0=op0, op1=op1), r, w)

        def STT(out, in0, sc, in1, op0, op1, r, w):
            return P.add("dve", lambda e: e.scalar_tensor_tensor(out=out, in0=in0, scalar=sc, in1=in1, op0=op0, op1=op1), r, w)

        def TT(eng, out, in0, in1, op, r, w):
            return P.add(eng, lambda e: e.tensor_tensor(out=out, in0=in0, in1=in1, op=op), r, w)

        def CP(eng, out, in_, r, w):
            if eng == "act":
                return ACT(out, in_, AF.Copy, r, w)
            return P.add(eng, lambda e: e.tensor_copy(out=out, in_=in_), r, w)

        def MS(eng, out, val, w):
            return P.add(eng, lambda e: e.memset(out, val), (), w)

        def RED(out, in_, r, w):
            return P.add("dve", lambda e: e.tensor_reduce(out=out, in_=in_, axis=AX.X, op=ALU.add), r, w)

        def MMG(mms, r, w):
            def fn(e):
                inst = None
                for (o, l, rh, st_, sp_) in mms:
                    inst = e.matmul(o, lhsT=l, rhs=rh, start=st_, stop=sp_, skip_group_check=True)
                return inst
            return P.add("pe", fn, r, w)

        def TRG(trs, r, w):
            def fn(e):
                inst = None
                for (o, i_, idn) in trs:
                    inst = e.transpose(out=o, in_=i_, identity=idn)
                return inst
            return P.add("pe", fn, r, w)

        def DMA(out, in_, r, w, extra=()):
            return P.add("sp", lambda e: e.dma_start(out=out, in_=in_), r, w, dma=True, extra=extra)

        _pp = [0]

        def nextP():
            _pp[0] = (_pp[0] + 1) % 3
            return psP[_pp[0]], ("psP", _pp[0])

        def bfv(pt):
            return pt[:].bitcast(BF16)

        MS("pool", ones_f[:], 1.0, ["ones_f"])
        MS("pool", ones_bf[:], 1.0, ["ones_bf"])
        MS("pool", tmpf, -1.0, ["sqs"])
        P.add("pool", lambda e: e.affine_select(out=ident_f[:], in_=ones_f[:], pattern=[[1, 128]], compare_op=ALU.is_equal,
                                                fill=0.0, base=0, channel_multiplier=-1), ["ones_f"], ["ident_f"])
        CP("dve", ident_bf[:], ident_f[:], ["ident_f"], ["ident_bf"])
        P.add("pool", lambda e: e.affine_select(out=TRI[:], in_=ones_bf[:], pattern=[[1, 128]], compare_op=ALU.is_gt,
                                                fill=0.0, base=0, channel_multiplier=-1), ["ones_bf"], ["TRI"])
        CP("dve", BLK[:], ones_bf[:], ["ones_bf"], ["BLK"])
        MS("dve", BLK[64:128, 0:64], 0.0, ["BLK"])
        CP("dve", negones[:], tmpf, ["sqs"], ["negones"])
        P.add("pool", lambda e: e.affine_select(out=ones_f[:], in_=tmpf, pattern=[[-1, 128]], compare_op=ALU.is_ge,
                                                fill=0.0, base=0, channel_multiplier=1), ["sqs", "ones_f"], ["ones_f"])
        CP("dve", negtri[:], ones_f[:], ["ones_f"], ["negtri"])
        MS("pool", ones_f[:], 1.0, ["ones_f"])
        for i, v in enumerate([1.0 / 1024, 1.0 / 64, 1.0 / 64, 1.0 / 64, 1.0 / 64, 1.0 / 256, 1.0 / 128, 1.0 / 32]):
            MS("dve", invn[:, i:i + 1], v, ["invn"])
        for i, v in enumerate([1.0 / 64, 1.0 / 32, 1.0 / 64, 1.0]):
            MS("dve", invn2[:, i:i + 1], v, ["invn2"])
        for i, v in enumerate([-4.0, -5.0, 0.0, -4.0]):
            MS("dve", shift[:, i:i + 1], v, ["shift"])
        MS("dve", HS[0][:], 0.0, ["HS0"])
        MS("dve", HS[0][0:64, :], 1.0, ["HS0"])
        MS("dve", HS[1][:], 0.0, ["HS1"])
        MS("dve", HS[1][64:128, :], 1.0, ["HS1"])
        CP("dve", MCs1[:], TRI[:, 64:128], ["TRI"], ["MCs1"])
        MS("dve", MCs1[0:64, :], 0.0, ["MCs1"])
        MS("dve", stat2[:], 1.0, ["stat2"])
        for vt_, vn_ in ((VA, "VA"), (VB, "VB"), (VC, "VC")):
            MS("pool", vt_[:, NB, :], 0.0, [(vn_, NB)])
        MS("pool", KTB[96:128, :], 0.0, [("KTB", j) for j in range(NB)])
        for i in range(NBUF):
            MS("pool", QTB[i][96:128, :], 0.0, [("QTB", i, u) for u in range(4)])
        for i in range(2):
            MS("dve", Rbb[i][:].bitcast(F32), 0.0, [("R", i)])

        def rstd_round(st_t, stm_t, ln_t, rs_t, inv_t, n, keys):
            ks, km, kl, kr_ = keys
            TT("dve", stm_t[:, 0:n], st_t[:, 0:n], inv_t[:, 0:n], ALU.mult, [ks], [km])
            ACT(ln_t[:, 0:n], stm_t[:, 0:n], AF.Ln, [km], [kl], bias=EPSB[:, 0:1])
            ACT(rs_t[:, 0:n], ln_t[:, 0:n], AF.Exp, [kl], [kr_], scale=-0.5)

        EPSB = sb("epsb", [128, 1], F32)
        MS("dve", EPSB[:], EPS, ["epsb"])

        def rope(src, cst, dst, rkeys, wkeys):
            TT("pool", rpa[:], src[:, 0:32], cst[:, 0:32], ALU.mult, rkeys, ["rpa"])
            TT("pool", rpb[:, 0:16], src[:, 16:32], cst[:, 32:48], ALU.mult, rkeys, ["rpb0"])
            TT("pool", rpb[:, 16:32], src[:, 0:16], cst[:, 48:64], ALU.mult, rkeys, ["rpb1"])
            TT("pool", dst, rpa[:], rpb[:], ALU.add, ["rpa", "rpb0", "rpb1"], wkeys)

        def mla_kv(ckv_f32, kr_f32, rk, blk, kcol0):
            CP("pool", ckvb[:], ckv_f32, rk, ["ckvb"])
            pt, pk = nextP()
            TRG([(bfv(pt)[:, 0:128], ckvb[:], ident_bf[:])], ["ckvb", "ident_bf"], [pk])
            CP("dve", ckvT[:], bfv(pt)[:, 0:128], [pk], ["ckvT"])
            pt2, pk2 = nextP()
            MMG([(pt2[:, 0:128], ckvT[:], wkv_sb[:], True, True)], ["ckvT", "wkv_sb"], [pk2])
            ACT(sq2[:, 0:64], pt2[:, 0:64], AF.Square, [pk2], ["sqs", "stat2c"], accum_out=stat2[:, 2:3])
            TSC("dve", statm2[:, 2:3], stat2[:, 2:3], 1.0 / 64, None, ALU.mult, None, ["stat2c"], ["statm2c"])
            ACT(lnv2[:, 2:3], statm2[:, 2:3], AF.Ln, ["statm2c"], ["lnv2c"], bias=EPSB[:, 0:1])
            ACT(rstd2[:, 2:3], lnv2[:, 2:3], AF.Exp, ["lnv2c"], ["rstd2c"], scale=-0.5)
            STT(kbtok[:, 0:64], pt2[:, 0:64], rstd2[:, 2:3], gvb[:, G_BKN:G_BKN + 64], ALU.mult, ALU.mult,
                [pk2, "rstd2c", "gvb"], ["kbtok_n"])
            CP("dve", VB[:, blk, :], pt2[:, 64:128], [pk2], [("VB", blk)])
            CP("pool", kbtok[:, 64:96], kr_f32, rk, ["kbtok_r"])
            pt3, pk3 = nextP()
            TRG([(bfv(pt3)[0:96, 0:128], kbtok[:], ident_bf[:])], ["kbtok_n", "kbtok_r", "ident_bf"], [pk3])
            jb = kcol0 // 128
            ali = ["wout_sb"] if jb < 64 else ([("YT", 0)] if jb < 96 else ([("YT", 1)] if jb < 128 else []))
            CP("dve", KTB[0:96, kcol0:kcol0 + 128], bfv(pt3)[0:96, 0:128], [pk3], [("KTB", jb)] + ali)

        def token_norm_T(src_f32, skey):
            ACT(junk[:], src_f32, AF.Square, [skey], ["junk", "stat_x"], accum_out=stat[:, 0:1])
            ACT(lnv[:, 0:1], stat[:, 0:1], AF.Ln, ["stat_x"], ["lnv_x"], scale=1.0 / 1024, bias=EPSB[:, 0:1])
            ACT(rstd[:, 0:1], lnv[:, 0:1], AF.Exp, ["lnv_x"], ["rstd_x"], scale=-0.5)
            TSC("dve", xn[:], src_f32, rstd[:, 0:1], None, ALU.mult, None, [skey, "rstd_x"], ["xn"])
            pt, pk = nextP()
            TRG([(bfv(pt)[:, 128 * k:128 * (k + 1)], xn[:, 128 * k:128 * (k + 1)], ident_bf[:]) for k in range(8)],
                ["xn", "ident_bf"], [pk])
            CP("dve", xnT[:], bfv(pt)[:, 0:1024], [pk], ["xnT"])

        prev_ag = None
        for l in range(L):
            xsrc = x_in if l == 0 else xmid
            xdst = xmid if l < L - 1 else y_out

            DMA(ngs[:], ng_d[l], [], ["ngs"])
            DMA(mngs[:], mng_d[l], [], ["mngs"])
            DMA(cqgs[:], cqg_d[l], [], ["cqgs"])
            DMA(gvb[:], gv_d[l].partition_broadcast(128), [], ["gvb"])
            DMA(outg_sb[:], outg_d[l], [], ["outg"])
            TSC("dve", gvb[:, 0:128], gvb[:, 0:128], 0.125, None, ALU.mult, None, ["gvb"], ["gvb"])
            TSC("dve", gvb[:, G_BQN:G_BQN + 96], gvb[:, G_BQN:G_BQN + 96], MLA_SCALE, None, ALU.mult, None, ["gvb"], ["gvb"])
            for k in range(8):
                w_ = XT[k % 2]
                wk = ("xt", k % 2)
                DMA(w_[:, 0:1024], w_in_d[l, 128 * k:128 * (k + 1), 0:1024], [], [wk])
                DMA(wsm[k % 2], w_in_d[l, 128 * k:128 * (k + 1), 1024:NCOL], [], ["sqs"])
                TSC("dve", w_in_sb[:, k, 0:1024], w_[:, 0:1024], ngs[:, k:k + 1], None, ALU.mult, None, [wk, "ngs"], ["w_in_sb"])
                TSC("dve", w_in_sb[:, k, 1024:NCOL], wsm[k % 2], ngs[:, k:k + 1], None, ALU.mult, None, ["sqs", "ngs"], ["w_in_sb"])
            for k in range(8):
                w_ = XT[k % 2]
                wk = ("xt", k % 2)
                DMA(w_[:, 0:128], wmem_d[l, 128 * k:128 * (k + 1), :], [], [wk])
                TSC("dve", wmem_sb[:, k, :], w_[:, 0:128], mngs[:, k:k + 1], None, ALU.mult, None, [wk, "mngs"], ["wmem_sb"])
            for k in range(2):
                w_ = XT[k % 2]
                wk = ("xt", k % 2)
                DMA(w_[:, 0:96], wq_d[l, 128 * k:128 * (k + 1), :], [], [wk])
                TSC("dve", wq_sb[:, k, :], w_[:, 0:96], cqgs[:, k:k + 1], None, ALU.mult, None, [wk, "cqgs"], ["wq_sb"])
            DMA(XT[0][:, 0:128], wkv_d[l], [], [("xt", 0)])
            CP("dve", wkv_sb[:], XT[0][:, 0:128], [("xt", 0)], ["wkv_sb"])

            P.add("dve", lambda e: e.memset(zrow[0:1, 0:768], 0.0), (), exkeys, extra=([prev_ag] if prev_ag is not None else []))
            DMA(exrow[0:1, 768:1281], tab_d[l:l + 1, :], [], exkeys)
            TSC("dve", exrow[0:1, 0:768], zrow[0:1, 0:768], exrow[0:1, 768:769], None, ALU.add, None, exkeys, exkeys)
            TSC("dve", exrow[0:1, 1281:2048], zrow[0:1, 0:767], exrow[0:1, 1280:1281], None, ALU.add, None, exkeys, exkeys)
            ACT(exrow[0:1, :], exrow[0:1, :], AF.Exp, exkeys, exkeys)
            DMA(ext_d[l:l + 1, :], exrow[0:1, :], exkeys, ["ext_d"] + exkeys)
            Jm = sqs[:, 0:128]
            P.add("pool", lambda e: e.affine_select(out=Jm, in_=ones_f[:], pattern=[[1, 128]], compare_op=ALU.is_equal,
                                                    fill=0.0, base=-127, channel_multiplier=1), ["ones_f", "sqs"], ["sqs"])
            for r in range(-4, 4):
                src = bass.AP(ext_d.tensor, l * 2048 + 1024 - 128 * r - 127, [[1, 128], [1, 512]])
                DMA(tz[:], src, ["ext_d"], [("E", 0)])
                ptz, pkz = nextP()
                MMG([(ptz[:, :], Jm, tz[:], True, True)], ["sqs", ("E", 0)], [pkz])
                MS("pool", vis[:], 0.0, [("W", 0)])
                for hk in range(2):
                    ck = 2 * r + hk
                    lo, hi = max(ck, 0), min(ck + 8, 7)
                    if lo <= hi:
                        MS("pool", vis[64 * hk:64 * hk + 64, 64 * lo:64 * (hi + 1)], 1.0, [("W", 0)])
                TT("dve", MA[r + 4][:], ptz[:, :], vis[:], ALU.mult, [pkz, ("W", 0)], [("MA", r + 4)])
            CP("dve", MAs1[:], MA[4][:, 64:128], [("MA", 4)], ["MAs1"])
            MS("dve", MAs1[0:64, :], 0.0, ["MAs1"])

            for mt in range(2):
                xt = XT[mt % 2]
                xk = ("xt", mt % 2)
                DMA(xt[:], mem_in[128 * mt:128 * (mt + 1), :], [], [xk])
                token_norm_T(xt[:], xk)
                pt, pk = nextP()
                MMG([(pt[:, 0:128], xnT[:, 128 * k:128 * (k + 1)], wmem_sb[:, k, :], k == 0, k == 7) for k in range(8)],
                    ["xnT", "wmem_sb"], [pk])
                ACT(sq2[:, 0:64], pt[:, 0:64], AF.Square, [pk], ["sqs", "stat2c"], accum_out=stat2[:, 2:3])
                TSC("dve", statm2[:, 2:3], stat2[:, 2:3], 1.0 / 64, None, ALU.mult, None, ["stat2c"], ["statm2c"])
                ACT(lnv2[:, 2:3], statm2[:, 2:3], AF.Ln, ["statm2c"], ["lnv2c"], bias=EPSB[:, 0:1])
                ACT(rstd2[:, 2:3], lnv2[:, 2:3], AF.Exp, ["lnv2c"], ["rstd2c"], scale=-0.5)
                STT(stm[:, 0:64], pt[:, 0:64], rstd2[:, 2:3], gvb[:, G_MK:G_MK + 64], ALU.mult, ALU.mult,
                    [pk, "rstd2c", "gvb"], ["stm_k"])
                CP("dve", stm[:, 64:128], pt[:, 64:128], [pk], ["stm_v"])
                DMA(mem_out[l, 128 * mt:128 * (mt + 1), :], stm[:], ["stm_k", "stm_v"], [("mem_out", l, mt)])
                CP("pool", kmtok[:], stm[:, 0:64], ["stm_k"], ["kmtok"])
                CP("pool", VM[:, mt, :], stm[:, 64:128], ["stm_v"], [("VM", mt)])
                pt2, pk2 = nextP()
                TRG([(bfv(pt2)[0:64, 0:128], kmtok[:], ident_bf[:])], ["kmtok", "ident_bf"], [pk2])
                CP("dve", KTM[0:64, 128 * mt:128 * (mt + 1)], bfv(pt2)[0:64, 0:128], [pk2], [("KTM", mt)])

            DMA(aks_out[l], c_ak[l, :, 64:512, :], [], [("aks", l)])
            DMA(avs_out[l], c_av[l, :, 64:512, :], [], [("avs", l)])

            def load_tile(i):
                xt = XT[i % 2]
                DMA(xt[:], xsrc[128 * i:128 * (i + 1), :], [("xdram", i)], [("xt", i % 2)])
                DMA(CS[i % 2][:], cs_in[128 * i:128 * (i + 1), :], [], [("cs", i % 2)])

            def project_tile(i, sp_, u):
                xt = XT[i % 2]
                xk = ("xt", i % 2)
                cst = CS[i % 2]
                ck_ = ("cs", i % 2)
                st = ST[i % len(ST)]
                sk = ("st", i % len(ST))
                blk = i
                c0 = 128 * blk
                token_norm_T(xt[:], xk)
                if i + 1 < NT:
                    load_tile(i + 1)
                yield
                g0, k0 = nextP()
                MMG([(g0[:, 0:448], xnT[:, 128 * k:128 * (k + 1)], w_in_sb[:, k, 0:448], k == 0, k == 7) for k in range(8)],
                    ["xnT", "w_in_sb"], [k0])
                g1, k1 = nextP()
                MMG([(g1[:, 0:416], xnT[:, 128 * k:128 * (k + 1)], w_in_sb[:, k, 448:864], k == 0, k == 7) for k in range(8)],
                    ["xnT", "w_in_sb"], [k1])
                yield
                ACT(sq0, g0[:, 0:256], AF.Square, [k0], ["sqs"])
                RED(stat[:, 1:5], sq0.rearrange("p (g d) -> p g d", d=64), ["sqs"], ["stat_a"])
                ACT(sq1[:, 0:416], g1[:, 0:416], AF.Square, [k1], ["sqs"])
                RED(stat[:, 5:6], sq1[:, 0:256], ["sqs"], ["stat_b"])
                RED(stat[:, 6:7], sq1[:, 256:384], ["sqs"], ["stat_c"])
                RED(stat[:, 7:8], sq1[:, 384:416], ["sqs"], ["stat_d"])
                TT("dve", statm[:, 1:8], stat[:, 1:8], invn[:, 1:8], ALU.mult, ["stat_a", "stat_b", "stat_c", "stat_d", "invn"], ["statm"])
                ACT(lnv[:, 1:8], statm[:, 1:8], AF.Ln, ["statm"], ["lnv_r"], bias=EPSB[:, 0:1])
                ACT(rstd[:, 1:8], lnv[:, 1:8], AF.Exp, ["lnv_r"], ["rstd_r"], scale=-0.5)
                yield
                STT(qtok[:, 0:64], g0[:, 0:64], rstd[:, 1:2], gvb[:, G_AQ:G_AQ + 64], ALU.mult, ALU.mult, [k0, "rstd_r", "gvb"], ["qtok_a"])
                TSC("dve", qtok[:, 64:128], g0[:, 64:128], 0.125, None, ALU.mult, None, [k0], ["qtok_c"])
                STT(qtok[:, 128:192], g0[:, 128:192], rstd[:, 3:4], gvb[:, G_MQ:G_MQ + 64], ALU.mult, ALU.mult, [k0, "rstd_r", "gvb"], ["qtok_m"])
                STT(st[:, 0:64], g0[:, 192:256], rstd[:, 4:5], gvb[:, G_AK:G_AK + 64], ALU.mult, ALU.mult, [k0, "rstd_r", "gvb"], [(sk, "ak")])
                CP("dve", st[:, 64:256], g0[:, 256:448], [k0], [(sk, "cav")])
                yield
                CP("pool", kactok[:], st[:, 0:128], [(sk, "ak"), (sk, "cav")], ["kactok"])
                CP("pool", VA[:, blk, :], st[:, 128:192], [(sk, "cav")], [("VA", blk)])
                CP("pool", VC[:, blk, :], st[:, 192:256], [(sk, "cav")], [("VC", blk)])
                yield
                TSC("dve", cqn[:], g1[:, 0:256], rstd[:, 5:6], None, ALU.mult, None, [k1, "rstd_r"], ["cqn"])
                STT(st[:, 256:384], g1[:, 256:384], rstd[:, 6:7], gvb[:, G_CKV:G_CKV + 128], ALU.mult, ALU.mult, [k1, "rstd_r", "gvb"], [(sk, "ckv")])
                STT(krn[:], g1[:, 384:416], rstd[:, 7:8], gvb[:, G_BKR:G_BKR + 32], ALU.mult, ALU.mult, [k1, "rstd_r", "gvb"], ["krn"])
                rope(krn, cst, st[:, 384:416], ["krn", ck_], [(sk, "kr")])
                yield
                g2, k2 = nextP()
                MMG([(g2[:, 0:256], xnT[:, 128 * k:128 * (k + 1)], w_in_sb[:, k, 864:1120], k == 0, k == 7) for k in range(8)],
                    ["xnT", "w_in_sb"], [k2])
                yield
                ACT(ge[:], g2[:, 0:256], AF.Exp, [k2], ["ge"], scale=-1.0)
                ACT(ge[:], ge[:], AF.Ln, ["ge"], ["ge"], bias=1.0)
                ACT(ge[:], ge[:], AF.Exp, ["ge"], ["ge"], scale=-1.0)
                TT("dve", gtok[:], g2[:, 0:256], ge[:], ALU.mult, [k2, "ge"], ["gtok"])
                yield
                pt, pk = nextP()
                TRG([(bfv(pt)[:, 128 * j:128 * (j + 1)], cqn[:, 128 * j:128 * (j + 1)], ident_bf[:]) for j in range(2)], ["cqn", "ident_bf"], [pk])
                CP("dve", cqnT[:], bfv(pt)[:, 0:256], [pk], ["cqnT"])
                yield
                qb, kq = nextP()
                MMG([(qb[:, 0:96], cqnT[:, 128 * j:128 * (j + 1)], wq_sb[:, j, :], j == 0, j == 1) for j in range(2)], ["cqnT", "wq_sb"], [kq])
                ACT(sq2, qb[:, 0:96], AF.Square, [kq], ["sqs"])
                RED(stat2[:, 0:1], sq2[:, 0:64], ["sqs"], ["stat2a"])
                RED(stat2[:, 1:2], sq2[:, 64:96], ["sqs"], ["stat2b"])
                TT("dve", statm2[:, 0:2], stat2[:, 0:2], invn2[:, 0:2], ALU.mult, ["stat2a", "stat2b", "invn2"], ["statm2ab"])
                ACT(lnv2[:, 0:2], statm2[:, 0:2], AF.Ln, ["statm2ab"], ["lnv2ab"], bias=EPSB[:, 0:1])
                ACT(rstd2[:, 0:2], lnv2[:, 0:2], AF.Exp, ["lnv2ab"], ["rstd2ab"], scale=-0.5)
                yield
                STT(qbtok[:, 0:64], qb[:, 0:64], rstd2[:, 0:1], gvb[:, G_BQN:G_BQN + 64], ALU.mult, ALU.mult, [kq, "rstd2ab", "gvb"], ["qbtok_n"])
                STT(qrn[:], qb[:, 64:96], rstd2[:, 1:2], gvb[:, G_BQR:G_BQR + 32], ALU.mult, ALU.mult, [kq, "rstd2ab", "gvb"], ["qrn"])
                rope(qrn, cst, qbtok[:, 64:96], ["qrn", ck_], ["qbtok_r"])
                yield
                mla_kv(st[:, 256:384], st[:, 384:416], [(sk, "ckv"), (sk, "kr")], blk, c0)
                yield
                DMA(st_out[l, 128 * i:128 * (i + 1), :], st[:], [(sk, "ak"), (sk, "cav"), (sk, "ckv"), (sk, "kr")], [("st_out", l, i)])
                pa, ka = nextP()
                TRG([(bfv(pa)[:, 0:128], qtok[:, 0:128], ident_bf[:]), (bfv(pa)[0:64, 128:256], qtok[:, 128:192], ident_bf[:]),
                     (bfv(pa)[:, 256:384], kactok[:], ident_bf[:]), (bfv(pa)[0:96, 384:512], qbtok[:], ident_bf[:])],
                    ["qtok_a", "qtok_c", "qtok_m", "kactok", "qbtok_n", "qbtok_r", "ident_bf"], [ka])
                CP("dve", QTAC[sp_][:, 128 * u:128 * (u + 1)], bfv(pa)[:, 0:128], [ka], [("QTAC", sp_, u)])
                CP("dve", QTM[sp_][0:64, 128 * u:128 * (u + 1)], bfv(pa)[0:64, 128:256], [ka], [("QTM", sp_, u)])
                CP("dve", KTAC[:, c0:c0 + 128], bfv(pa)[:, 256:384], [ka], [("KTAC", blk)])
                CP("dve", QTB[sp_][0:96, 128 * u:128 * (u + 1)], bfv(pa)[0:96, 384:512], [ka], [("QTB", sp_, u)])
                yield
                pg, kg = nextP()
                TRG([(bfv(pg)[0:64, 128 * g:128 * (g + 1)], gtok[:, 64 * g:64 * (g + 1)], ident_bf[:]) for g in range(4)], ["gtok", "ident_bf"], [kg])
                CP("dve", GT[sp_][0:64, :, 128 * u:128 * (u + 1)], bfv(pg)[0:64, 0:512].rearrange("p (g c) -> p g c", g=4), [kg], [("GT", sp_, u)])

            pending = []
            pump_k = [1]

            def pump():
                for _ in range(pump_k[0]):
                    while pending:
                        try:
                            next(pending[0])
                            break
                        except StopIteration:
                            pending.pop(0)

            def drain():
                while pending:
                    try:
                        next(pending[0])
                    except StopIteration:
                        pending.pop(0)

            _sc = [0, 0, 0]

            def softmax_blocks(grp, sp_, qt, qrows, blocks, kt, vt, vkey, ktkey, first, last_flag):
                qkeys = [(qt[1], sp_, u) for u in range(4)]
                n = len(blocks)
                base_s = _sc[0]
                base_p = _sc[1]
                _sc[0] += n
                _sc[1] += n

                def S(b):
                    kc0, vb, c0, c1 = blocks[b][0:4]
                    si = (base_s + b) % 2
                    MMG([(psS[si][:, c0:c1], kt[qrows[0]:qrows[1], kc0:kc0 + 128], qt[0][sp_][qrows[0]:qrows[1], c0:c1], True, True)],
                        [(ktkey, kc0 // 128)] + qkeys, [("psS", si)])

                def E(b):
                    kc0, vb, c0, c1, mask, mkey, mc0, mc1 = blocks[b]
                    si = (base_s + b) % 2
                    pi = (base_p + b) % len(Pb)
                    ACT(Pb[pi][:, c0:c1], psS[si][:, c0:c1], AF.Exp, [("psS", si), "shift"], [("P", pi)], bias=shift[:, grp:grp + 1])
                    if mask is not None:
                        meng = "dve" if (mc1 - mc0) >= 512 else "pool"
                        TT(meng, Pb[pi][:, mc0:mc1], Pb[pi][:, mc0:mc1], mask, ALU.mult, [("P", pi), mkey], [("P", pi)])

                def V(b):
                    kc0, vb, c0, c1 = blocks[b][0:4]
                    pi = (base_p + b) % len(Pb)
                    st_ = first and b == 0
                    sp2 = last_flag and b == n - 1
                    if vkey == "VM":
                        MMG([(psO[0:64, c0:c1], vt[:, vb, :], Pb[pi][:, c0:c1], st_, sp2)], [(vkey, vb), ("P", pi)], ["psO"])
                    else:
                        MMG([(psO[:, c0:c1], vt[:, vb:vb + 2, :].rearrange("p a b -> p (a b)"), Pb[pi][:, c0:c1], st_, sp2)],
                            [(vkey, vb), ("P", pi)], ["psO"])
                    if b == 0 and first:
                        CP("dve", Pacc[:, c0:c1], Pb[pi][:, c0:c1], [("P", pi)], ["Pacc"])
                    else:
                        TT("dve", Pacc[:, c0:c1], Pacc[:, c0:c1], Pb[pi][:, c0:c1], ALU.add, ["Pacc", ("P", pi)], ["Pacc"])
                    if sp2:
                        MMG([(psD[0:64, :], ones_f[:, 0:64], Pacc[:], True, True)], ["ones_f", "Pacc"], ["psD"])

                S(0)
                for b in range(n + 1):
                    if b < n:
                        E(b)
                    if b + 1 < n:
                        S(b + 1)
                    if b >= 1:
                        V(b - 1)
                    pump()

            def stick_blocks(sp_, blocks, first, last_flag):
                qkeys = [("QTAC", sp_, u) for u in range(4)]
                n = len(blocks)
                base = _sc[2]
                _sc[2] += n
                base_s = _sc[0]
                _sc[0] += n
                accb = [(psAcc, "psAcc"), (psD, "psD")]

                def ops(i):
                    kc0, vb, c0, c1 = blocks[i][0:4]
                    return KTAC[:, kc0:kc0 + 128], QTAC[sp_][:, c0:c1]

                def Z(i):
                    kc0, vb, c0, c1 = blocks[i][0:4]
                    si = (base_s + i) % 2
                    lhs, rhs = ops(i)
                    MMG([(psS[si][:, c0:c1], lhs, rhs, True, True)], [("KTAC", kc0 // 128)] + qkeys, [("psS", si)])

                def X1(i):
                    kc0, vb, c0, c1 = blocks[i][0:4]
                    si = (base_s + i) % 2
                    ei = (base + i) % 2
                    ACT(Eb[ei % NB1][:, c0:c1], psS[si][:, c0:c1], AF.Exp, [("psS", si)], [("E", ei % NB1)])

                def LN(i):
                    kc0, vb, c0, c1, mask, mkey, mc0, mc1 = blocks[i]
                    ei = (base + i) % 2
                    ACT(SPb[ei % NB1][:, c0:c1], Eb[ei % NB1][:, c0:c1], AF.Ln, [("E", ei % NB1)], [("SP", ei % NB1)], bias=1.0)
                    if mask is not None:
                        TT("dve", SPb[ei % NB1][:, mc0:mc1], SPb[ei % NB1][:, mc0:mc1].bitcast(F32), mask, ALU.mult, [("SP", ei % NB1), mkey], [("SP", ei % NB1)])

                def AC(i):
                    kc0, vb, c0, c1 = blocks[i][0:4]
                    ei = (base + i) % 2
                    pa, pak = accb[(base + i) % 2]
                    lhs, rhs = ops(i)
                    f0 = first and i == 0
                    mms = [(pa[:, c0:c1], lhs, rhs, True, False), (pa[:, c0:c1], negtri[:], SPb[ei % NB1][:, c0:c1], False, f0)]
                    rk = [("KTAC", kc0 // 128), ("SP", ei % NB1), "negtri"] + qkeys
                    if not f0:
                        mms.append((pa[:, c0:c1], negones[:], Rbb[ei][:, c0:c1], False, True))
                        rk += [("R", ei), "negones"]
                    MMG(mms, rk, [pak])

                def RU(i):
                    kc0, vb, c0, c1 = blocks[i][0:4]
                    ei = (base + i) % 2
                    TT("dve", Rbb[1 - ei][:, c0:c1], Rbb[ei][:, c0:c1].bitcast(F32), SPb[ei % NB1][:, c0:c1].bitcast(F32), ALU.add,
                       [("R", ei), ("SP", ei % NB1)], [("R", 1 - ei)])

                def X2(i):
                    kc0, vb, c0, c1, mask, mkey, mc0, mc1 = blocks[i]
                    ei = (base + i) % 2
                    pa, pak = accb[(base + i) % 2]
                    ACT(Wb[ei % NB1][:, c0:c1], pa[:, c0:c1], AF.Exp, [pak], [("W", ei % NB1)])
                    if mask is not None:
                        TT("pool", Wb[ei % NB1][:, mc0:mc1], Wb[ei % NB1][:, mc0:mc1], mask, ALU.mult, [("W", ei % NB1), mkey], [("W", ei % NB1)])

                def PV(i):
                    kc0, vb, c0, c1 = blocks[i][0:4]
                    ei = (base + i) % 2
                    MMG([(psO[:, c0:c1], VC[:, vb:vb + 2, :].rearrange("p a b -> p (a b)"), Wb[ei % NB1][:, c0:c1], first and i == 0, last_flag and i == n - 1)],
                        [("VC", vb), ("W", ei % NB1)], ["psO"])

                Z(0)
                if n > 1:
                    Z(1)
                X1(0)
                for i in range(n + 1):
                    if i < n:
                        LN(i)
                    if i + 2 < n:
                        Z(i + 2)
                    if i >= 2:
                        PV(i - 2)
                    if i + 1 < n:
                        X1(i + 1)
                    if i < n:
                        AC(i)
                        if i + 1 < n:
                            RU(i)
                    if i >= 1:
                        X2(i - 1)
                    pump()
                PV(n - 1)

            def zero_R():
                for i in range(2):
                    TSC("dve", Rbb[i][:], Rbb[i][:].bitcast(F32), 0.0, None, ALU.mult, None, [("R", i)], [("R", i)])

            def post_group(grp, sp_, t, softmax, ybi):
                cols = slice(512 * t, 512 * (t + 1))
                ACT(psq[:], psO[0:64, :], AF.Square, ["psO"], ["psq"])
                si_ = _sc[0] % 2
                _sc[0] += 1
                pt, pk = psS[si_], ("psS", si_)
                MMG([(pt[0:64, :], ones_f[0:64, 0:64], psq[:], True, True)], ["ones_f", "psq"], [pk])
                if softmax:
                    CP("dve", pd[:], psD[0:64, :], ["psD"], ["pd"])
                    STT(pd[:], pd[:], EPS, pd[:], ALU.mult, ALU.mult, ["pd"], ["pd"])
                    STT(pd[:], pt[0:64, :], 1.0 / 64, pd[:], ALU.mult, ALU.add, [pk, "pd"], ["pd"])
                else:
                    TSC("dve", pd[:], pt[0:64, :], 1.0 / 64, EPS, ALU.mult, ALU.add, [pk], ["pd"])
                ACT(pd[:], pd[:], AF.Ln, ["pd"], ["pd"])
                ACT(pd[:], pd[:], AF.Exp, ["pd"], ["pd"], scale=-0.5)
                STT(psq[:], psO[0:64, :], outg_sb[0:64, grp:grp + 1], pd[:], ALU.mult, ALU.mult, ["psO", "outg", "pd"], ["psq"])
                yt = yTb[ybi]
                TT("pool", yt[:], psq[:], GT[sp_][0:64, grp, :], ALU.mult, ["psq"] + [("GT", sp_, u) for u in range(4)], [("yT", ybi)])
                yops.append(DMA(yT_loc[256 * t + 64 * grp:256 * t + 64 * (grp + 1), :], yt[:], [("yT", ybi)], [("yT_loc", grp, t)]))

            _yb = [0]

            def ybuf():
                _yb[0] += 1
                return _yb[0] % len(yTb)

            def attention_prompt(t, sp_):
                nb = 4 * t + 4
                blocks = []
                for r in range(-4, 4):
                    j = 4 * t + r
                    if j < 0:
                        continue
                    blocks.append((128 * j, j, 0, 512, MA[r + 4][:], ("MA", r + 4), 0, 512))
                softmax_blocks(0, sp_, (QTAC, "QTAC"), (0, 64), blocks, KTAC, VA, "VA", "KTAC", True, True)
                post_group(0, sp_, t, True, ybuf())
                blocks = [(128 * j, j, 0, 512, None, None, 0, 0) for j in range(4 * t)]
                for r in range(4):
                    j = 4 * t + r
                    blocks.append((128 * j, j, 128 * r, 512, BLK[:], "BLK", 128 * r, 128 * r + 128))
                softmax_blocks(1, sp_, (QTB, "QTB"), (0, 128), blocks, KTB, VB, "VB", "KTB", True, True)
                post_group(1, sp_, t, True, ybuf())
                MS("pool", QTAC[sp_][0:64, :], 0.0, [("QTAC", sp_, u) for u in range(4)])
                zero_R()
                blocks = []
                for r in range(3, -1, -1):
                    j = 4 * t + r
                    blocks.append((128 * j, j, 128 * r, 512, TRI[:], "TRI", 128 * r, 128 * r + 128))
                for j in range(4 * t - 1, -1, -1):
                    blocks.append((128 * j, j, 0, 512, None, None, 0, 0))
                stick_blocks(sp_, blocks, True, True)
                post_group(2, sp_, t, False, ybuf())
                blocks = [(128 * j, j, 0, 512, None, None, 0, 0) for j in range(2)]
                softmax_blocks(3, sp_, (QTM, "QTM"), (0, 64), blocks, KTM, VM, "VM", "KTM", True, True)
                post_group(3, sp_, t, True, ybuf())

            def attention_sample(t, sp_):
                sq_base = (t - NPS) * 8
                for s8 in range(8):
                    s = sq_base + s8
                    c0, c1 = 64 * s8, 64 * s8 + 64
                    hf = s % 2
                    nblk_i = NPT + s // 2
                    blocks = [(128 * (CB0 + 8 * s + i), CB0 + 8 * s + i, c0, c1, MA[i][:, 0:64], ("MA", i), c0, c1) for i in range(4)]
                    blocks.append((128 * nblk_i, nblk_i, c0, c1, (MA[4][:, 0:64] if hf == 0 else MAs1[:]),
                                   (("MA", 4) if hf == 0 else "MAs1"), c0, c1))
                    softmax_blocks(0, sp_, (QTAC, "QTAC"), (0, 64), blocks, KTAC, VA, "VA", "KTAC", True, s8 == 7)
                post_group(0, sp_, t, True, ybuf())
                for s8 in range(8):
                    s = sq_base + s8
                    c0, c1 = 64 * s8, 64 * s8 + 64
                    hf = s % 2
                    nblk_i = NPT + s // 2
                    blocks = [(128 * (CB0 + 8 * s + i), CB0 + 8 * s + i, c0, c1, None, None, 0, 0) for i in range(8)]
                    blocks.append((128 * nblk_i, nblk_i, c0, c1, HS[hf][:], f"HS{hf}", c0, c1))
                    softmax_blocks(1, sp_, (QTB, "QTB"), (0, 128), blocks, KTB, VB, "VB", "KTB", True, s8 == 7)
                post_group(1, sp_, t, True, ybuf())
                MS("pool", QTAC[sp_][0:64, :], 0.0, [("QTAC", sp_, u) for u in range(4)])
                zero_R()
                for s8 in range(8):
                    s = sq_base + s8
                    c0, c1 = 64 * s8, 64 * s8 + 64
                    hf = s % 2
                    nblk_i = NPT + s // 2
                    blocks = [(128 * nblk_i, nblk_i, c0, c1, (TRI[:, 0:64] if hf == 0 else MCs1[:]), ("TRI" if hf == 0 else "MCs1"), c0, c1)]
                    blocks += [(128 * (CB0 + 8 * s + i), CB0 + 8 * s + i, c0, c1, None, None, 0, 0) for i in range(7, -1, -1)]
                    stick_blocks(sp_, blocks, True, s8 == 7)
                post_group(2, sp_, t, False, ybuf())
                for s8 in range(8):
                    s = sq_base + s8
                    c0, c1 = 64 * s8, 64 * s8 + 64
                    blocks = [(128 * (MB0 + 2 * s8 + i), MB0 + 2 * s8 + i, c0, c1, None, None, 0, 0) for i in range(2)]
                    softmax_blocks(3, sp_, (QTM, "QTM"), (0, 64), blocks, KTM, VM, "VM", "KTM", True, s8 == 7)
                post_group(3, sp_, t, True, ybuf())

            def cache_prep(s_lo, s_hi):
                r2 = lambda ap: ap.rearrange("(i p) d -> p i d", p=128)
                for s in range(s_lo, s_hi):
                    for c4 in range(4):
                        rows = slice(256 * c4, 256 * c4 + 256)
                        if c4 < 2:
                            DMA(kcst[:, :, 0:64], r2(c_ak[l, s, rows, :]), [], ["kcst_a"])
                            DMA(vast[:], r2(c_av[l, s, rows, :]), [], ["vast"])
                        DMA(kcst[:, :, 64:128], r2(c_ck[l, s, rows, :]), [], ["kcst_c"])
                        DMA(vcst[:], r2(c_cv[l, s, rows, :]), [], ["vcst"])
                        DMA(ckvst[:], r2(c_ckv[l, s, rows, :]), [], ["ckvst"])
                        DMA(krst[:], r2(c_kr[l, s, rows, :]), [], ["krst"])
                        for i2 in range(2):
                            i = 2 * c4 + i2
                            blk = CB0 + 8 * s + i
                            pt, pk = nextP()
                            P.add("pe", (lambda e, o=pt[:, 0:128], a_=kcst[:, i2, :]: e.transpose(out=o, in_=a_, identity=ident_f[:])),
                                  ["kcst_a", "kcst_c", "ident_f"], [pk])
                            CP("dve", KTAC[:, 128 * blk:128 * blk + 128], pt[:, 0:128], [pk], [("KTAC", blk)])
                            if i < 4:
                                CP("pool", VA[:, blk, :], vast[:, i2, :], ["vast"], [("VA", blk)])
                            CP("pool", VC[:, blk, :], vcst[:, i2, :], ["vcst"], [("VC", blk)])
                            mla_kv(ckvst[:, i2, :], krst[:, i2, :], ["ckvst", "krst"], blk, 128 * blk)
                    DMA(mkst[:], r2(c_mk[l, s]), [], ["mkst"])
                    DMA(mvst[:], r2(c_mv[l, s]), [], ["mvst"])
                    for i in range(2):
                        pt, pk = nextP()
                        P.add("pe", (lambda e, o=pt[0:64, 0:128], a_=mkst[:, i, :]: e.transpose(out=o, in_=a_, identity=ident_f[:])),
                              ["mkst", "ident_f"], [pk])
                        mb = MB0 + 2 * (s % 8) + i
                        CP("dve", KTM[0:64, 128 * mb:128 * mb + 128], pt[0:64, 0:128], [pk], [("KTM", mb)])
                        CP("pool", VM[:, mb, :], mvst[:, i, :], ["mvst"], [("VM", mb)])

            yops = []
            ag_ops = []
            load_tile(0)
            for u in range(4):
                pending.append(project_tile(u, 0, u))
            drain()
            for t in range(NSUP):
                sp_ = t % NBUF
                if t >= NPS:
                    cache_prep(8 * (t - NPS), 8 * (t - NPS) + 8)
                if t + 1 < NSUP and NBUF == 2:
                    for u in range(4):
                        pending.append(project_tile(4 * (t + 1) + u, (t + 1) % NBUF, u))
                    n_it = (min(8, 4 * t + 4) + 2 * (4 * t + 4) + 2 + 4) if t < NPS else 216
                    pump_k[0] = max(1, -(-60 // n_it))
                del yops[:]
                if t < NPS:
                    attention_prompt(t, sp_)
                else:
                    attention_sample(t, sp_)
                ag_t = Op("pool", None, True)
                ag_t.inc = 1
                ag_t.sem = ccsem
                ag_t.deps = set(yops)
                ag_t.fn = (lambda e, t=t: e.collective_compute(
                    "AllGather", ALU.bypass, replica_groups=[[0, 1, 2, 3], [4, 5, 6, 7]],
                    ins=[yT_loc[256 * t:256 * (t + 1), :]], outs=[yT_all[1024 * t:1024 * (t + 1), :]]))
                P.ops["pool"].append(ag_t)
                P.order.append(ag_t)
                ag_ops.append(ag_t)
                drain()
                if t + 1 < NSUP and NBUF == 1:
                    for u in range(4):
                        pending.append(project_tile(4 * (t + 1) + u, 0, u))
                    drain()

            ag = ag_ops[-1]

            prev_ag = ag
            for k in range(8):
                w_ = XT[k % 2]
                wk = ("xt", k % 2)
                DMA(w_[:, 0:1024], wout_d[l, 128 * k:128 * (k + 1), :], [], [wk], extra=[ag])
                TSC("dve", wout_sb[:, k, :], w_[:, 0:1024], 1.0, None, ALU.mult, None, [wk], ["wout_sb"])
            bankrot = [(psS[0], ("psS", 0)), (psS[1], ("psS", 1)), (psAcc, "psAcc"), (psO, "psO"), (psD, "psD")]
            bi_ = 0
            for t in range(NSUP):
                yt_ = YTs[t % 2]
                DMA(yt_, yT_all[1024 * t:1024 * (t + 1), :].rearrange("(k p) c -> p k c", p=128), [], [("YT", t % 2)], extra=[ag_ops[t], ag])
                for u in range(4):
                    i = 4 * t + u
                    xo = XO[i % 2]
                    DMA(xo[:], xsrc[128 * i:128 * (i + 1), :], [("xdram", i)], [("xt", i % 2)])
                    for half in range(2):
                        pb, pbk = bankrot[bi_ % 5]
                        bi_ += 1
                        MMG([(pb[:, :], yt_[:, k, 128 * u:128 * (u + 1)], wout_sb[:, k, 512 * half:512 * (half + 1)], k == 0, k == 7)
                             for k in range(8)], [("YT", t % 2), "wout_sb"], [pbk])
                        TT("dve", xo[:, 512 * half:512 * (half + 1)], pb[:, :], xo[:, 512 * half:512 * (half + 1)], ALU.add,
                           [pbk, ("xt", i % 2)], [("xt", i % 2)])
                    if l < L - 1:
                        DMA(xdst[128 * i:128 * (i + 1), :], xo[:], [("xt", i % 2)], [("xdram", i)])
                    else:
                        DMA(xdst[128 * i:128 * (i + 1), :], xo[:], [("xt", i % 2)], [("yout", i)])

        P.finalize(esem, dsems)
        with nc.Block() as block:
            @block.sync
            def _(e):
                P.emit("sp", e)

            @block.scalar
            def _(e):
                P.emit("act", e)

            @block.vector
            def _(e):
                P.emit("dve", e)

            @block.gpsimd
            def _(e):
                P.emit("pool", e)

            @block.tensor
            def _(e):
                P.emit("pe", e)
        print("ops", len(P.order), {e: len(v) for e, v in P.ops.items()}, flush=True)
    return nc


IN_OFF = dict(aq=0, ak=256, av=512, ag=768, bcq=1024, bckv=1280, bkr=1408, bg=1440, cq=1696, ck=1952, cv=2208, cg=2464,
              mq=2720, mg=2976)


def _cols(h):
    def hd(name):
        o = IN_OFF[name] + 64 * h
        return list(range(o, o + 64))
    c = hd("aq") + hd("cq") + hd("mq") + hd("ak") + hd("ck") + hd("av") + hd("cv")
    c += list(range(1024, 1280)) + list(range(1280, 1408)) + list(range(1408, 1440))
    c += hd("ag") + hd("bg") + hd("cg") + hd("mg")
    return np.asarray(c)


def _rope_table(S):
    half = 16
    freqs = (np.float32(10000.0) ** (-np.arange(half, dtype=np.float32) / np.float32(half))).astype(np.float32)
    pos = np.concatenate([np.arange(S), np.tile(1024 + np.arange(TS), NSEQ)]).astype(np.float32)
    ang = (pos[:, None] * freqs[None, :]).astype(np.float32)
    c, s = np.cos(ang).astype(np.float32), np.sin(ang).astype(np.float32)
    return np.ascontiguousarray(np.concatenate([c, c, -s, s], 1))


def make_in_maps(inp, S, L):
    f = lambda a: np.ascontiguousarray(np.asarray(a, dtype=np.float32))
    cs = _rope_table(S)
    maps = []
    perm = np.asarray([g * 256 + r * 64 + d for r in range(4) for g in range(4) for d in range(64)])
    wout = f(np.asarray(inp["w_out"])[:L][:, perm, :])
    for c in range(8):
        g, h = c // 4, c % 4
        sl = slice(NSEQ * g, NSEQ * g + NSEQ)
        cols = _cols(h)
        gv = np.concatenate([inp["a_qn_g"][:L], inp["m_qn_g"][:L], inp["a_kn_g"][:L], inp["b_qn_g"][:L], inp["b_qr_g"][:L],
                             inp["b_kn_g"][:L], inp["b_kr_g"][:L], inp["b_ckv_g"][:L], inp["m_kn_g"][:L]], axis=1)
        og = np.asarray(inp["out_g"])[:L].reshape(L, 4, 4, 64)[:, :, h, :].transpose(0, 2, 1)
        wm = np.asarray(inp["w_mem_kv"])[:L]
        m = dict(
            x=f(np.concatenate([np.asarray(inp["x_prompt"])[g, :S], np.asarray(inp["x_sample"])[sl].reshape(NSEQ * TS, D)], 0)),
            mem=f(inp["mem_prompt"][g]),
            cs=cs,
            w_in=f(np.asarray(inp["w_in"])[:L][:, :, cols]),
            ng=f(np.asarray(inp["norm_g"])[:L].reshape(L, 8, 128).transpose(0, 2, 1)),
            wq=f(np.asarray(inp["b_wq_b"])[:L][:, :, 96 * h:96 * h + 96]),
            cqg=f(np.asarray(inp["b_cq_g"])[:L].reshape(L, 2, 128).transpose(0, 2, 1)),
            wkv=f(np.asarray(inp["b_wkv_b"])[:L][:, :, 128 * h:128 * h + 128]),
            wmem=f(np.concatenate([wm[:, :, 64 * h:64 * h + 64], wm[:, :, 256 + 64 * h:256 + 64 * h + 64]], 2)),
            mng=f(np.asarray(inp["m_norm_g"])[:L].reshape(L, 8, 128).transpose(0, 2, 1)),
            wout=wout,
            gv=f(gv),
            outg=f(og),
            tab=f(np.asarray(inp["a_rel_bias"])[:L][:, h, :]),
            c_ak=f(np.asarray(inp["cache_a_k"])[:L][:, sl, :, h, :]),
            c_av=f(np.asarray(inp["cache_a_v"])[:L][:, sl, :, h, :]),
            c_ck=f(np.asarray(inp["cache_c_k"])[:L][:, sl, :, h, :]),
            c_cv=f(np.asarray(inp["cache_c_v"])[:L][:, sl, :, h, :]),
            c_ckv=f(np.asarray(inp["cache_b_ckv"])[:L][:, sl]),
            c_kr=f(np.asarray(inp["cache_b_krope"])[:L][:, sl]),
            c_mk=f(np.asarray(inp["cache_mem_k"])[:L][:, sl, :, h, :]),
            c_mv=f(np.asarray(inp["cache_mem_v"])[:L][:, sl, :, h, :]),
        )
        maps.append(m)
    return maps


def assemble(res, S, L):
    B = 2
    DB = 2 * NSEQ
    o = lambda c, n: np.asarray(res[c][n])
    y_p = np.stack([o(4 * g, "y")[:S] for g in range(B)])
    y_s = np.concatenate([o(4 * g, "y")[S:].reshape(NSEQ, TS, D) for g in range(B)], 0)
    keep = min(512, S)
    a_k_p = np.zeros((L, B, keep, 4, 64), np.float32)
    a_v_p = np.zeros_like(a_k_p)
    ckv_p = np.zeros((L, B, S, 128), np.float32)
    kr_p = np.zeros((L, B, S, 32), np.float32)
    c_k_p = np.zeros((L, B, S, 4, 64), np.float32)
    c_v_p = np.zeros_like(c_k_p)
    m_k_p = np.zeros((L, B, 256, 4, 64), np.float32)
    m_v_p = np.zeros_like(m_k_p)
    a_k_s = np.zeros((L, DB, 512, 4, 64), np.float32)
    a_v_s = np.zeros_like(a_k_s)
    ckv_s = np.zeros((L, DB, TS, 128), np.float32)
    kr_s = np.zeros((L, DB, TS, 32), np.float32)
    c_k_s = np.zeros((L, DB, TS, 4, 64), np.float32)
    c_v_s = np.zeros_like(c_k_s)
    for c in range(8):
        g, h = c // 4, c % 4
        st = o(c, "o_state")
        sp = st[:, :S]
        ss = st[:, S:].reshape(L, NSEQ, TS, NST)
        sl = slice(NSEQ * g, NSEQ * g + NSEQ)
        a_k_p[:, g, :, h, :] = sp[:, S - keep:, 0:64]
        a_v_p[:, g, :, h, :] = sp[:, S - keep:, 128:192]
        c_k_p[:, g, :, h, :] = sp[:, :, 64:128]
        c_v_p[:, g, :, h, :] = sp[:, :, 192:256]
        mm = o(c, "o_mem")
        m_k_p[:, g, :, h, :] = mm[:, :, 0:64]
        m_v_p[:, g, :, h, :] = mm[:, :, 64:128]
        a_k_s[:, sl, 0:448, h, :] = o(c, "o_aks")
        a_v_s[:, sl, 0:448, h, :] = o(c, "o_avs")
        a_k_s[:, sl, 448:512, h, :] = ss[:, :, :, 0:64]
        a_v_s[:, sl, 448:512, h, :] = ss[:, :, :, 128:192]
        c_k_s[:, sl, :, h, :] = ss[:, :, :, 64:128]
        c_v_s[:, sl, :, h, :] = ss[:, :, :, 192:256]
        if h == 0:
            ckv_p[:, g] = sp[:, :, 256:384]
            kr_p[:, g] = sp[:, :, 384:416]
            ckv_s[:, sl] = ss[:, :, :, 256:384]
            kr_s[:, sl] = ss[:, :, :, 384:416]
    return (y_p, y_s, a_k_p, a_v_p, ckv_p, kr_p, c_k_p, c_v_p, m_k_p, m_v_p, a_k_s, a_v_s, ckv_s, kr_s, c_k_s, c_v_s)


def run(inp, S, L):
    nc = build(S, L)
    maps = make_in_maps(inp, S, L)
    res = run_bass_kernel_spmd(nc, maps, core_ids=list(range(8)))
    return assemble(res.results, S, L)


def kernel(**inputs):
    inp = {k: np.asarray(v) for k, v in inputs.items()}
    return run(inp, 16384, 2)
```
